# Optimizing a Trainium2 kernel written in Bass

```python
import jax, jax.numpy as jnp
from jax import lax
import numpy as np

D_MODEL = 1024
BATCH = 8
SEQ = 4096
DEPTH = 1

GRID_W = 64
CTX_LEN = 256
D_MIX = D_MODEL
RET_HEADS = 4
RET_DK = 128
RET_DV = 128
RET_W = RET_HEADS * RET_DV
RET_CHUNK = 128
ATT_HEADS = 8
ATT_KV_HEADS = 2
ATT_DH = 64
ATT_W = ATT_HEADS * ATT_DH
ATT_BLOCK = 128
WINDOW = 128
ROPE_BASE = 10000.0
EPS = 1e-6
NEG = -1e30
SPLIT_SIZES = (RET_HEADS * RET_DK, RET_HEADS * RET_DK, RET_W, RET_W,
               ATT_W, ATT_KV_HEADS * ATT_DH, ATT_KV_HEADS * ATT_DH, ATT_W)
IN_COLS = 4 * RET_W + 2 * ATT_W + 2 * ATT_KV_HEADS * ATT_DH

kernel_name = "hymba_retention_swa_sink_prefix_dit"


def rms_norm(x):
    x32 = x.astype(jnp.float32)
    return (x32 * lax.rsqrt(jnp.mean(x32 * x32, axis=-1, keepdims=True) + EPS)).astype(x.dtype)


def split_heads(t, n_heads):
    B, L, _ = t.shape
    return t.reshape(B, L, n_heads, -1).transpose(0, 2, 1, 3)


def merge_heads(t):
    B, H, L, d = t.shape
    return t.transpose(0, 2, 1, 3).reshape(B, L, H * d)


def project(h, w_in_l):
    y = h @ w_in_l
    idx = [int(i) for i in np.cumsum(SPLIT_SIZES)[:-1]]
    rq, rk, rv, rg, aq, ak, av, ag = jnp.split(y, idx, axis=-1)
    rq = split_heads(rq, RET_HEADS)
    rk = split_heads(rk, RET_HEADS) * (RET_DK ** -0.5)
    rv = split_heads(rv, RET_HEADS)
    aq = split_heads(aq, ATT_HEADS)
    ak = split_heads(ak, ATT_KV_HEADS)
    av = split_heads(av, ATT_KV_HEADS)
    return rq, rk, rv, rg, aq, ak, av, ag


def axial_rope(x, rows, cols):
    dh = x.shape[-1]
    half = dh // 2
    nf = half // 2
    inv = ROPE_BASE ** (-jnp.arange(nf, dtype=jnp.float32) / nf)
    ang = jnp.concatenate([rows.astype(jnp.float32)[:, None] * inv,
                           cols.astype(jnp.float32)[:, None] * inv], axis=-1)
    cos, sin = jnp.cos(ang), jnp.sin(ang)
    x1, x2 = x[..., :half], x[..., half:]
    return jnp.concatenate([x1 * cos - x2 * sin, x1 * sin + x2 * cos], axis=-1).astype(x.dtype)


def retention_chunked(q, k, v, log_gamma, state0, strict):
    B, H, L, _ = q.shape
    C = RET_CHUNK
    n = L // C

    def chunks(t):
        return t.astype(jnp.float32).reshape(B, H, n, C, t.shape[-1]).transpose(2, 0, 1, 3, 4)

    pos = jnp.arange(C, dtype=jnp.float32)
    diff = pos[:, None] - pos[None, :]
    mask = (diff > 0) if strict else (diff >= 0)
    decay_in = jnp.where(mask[None], jnp.exp(log_gamma[:, None, None] * jnp.where(mask, diff, 0.0)[None]), 0.0)
    decay_q = jnp.exp(log_gamma[:, None] * (pos + 1.0))[None, :, :, None]
    decay_k = jnp.exp(log_gamma[:, None] * (C - 1.0 - pos))[None, :, :, None]
    decay_c = jnp.exp(log_gamma * C)[None, :, None, None]

    def step(R, inp):
        qc, kc, vc = inp
        s = jnp.einsum('bhid,bhjd->bhij', qc, kc) * decay_in
        inner = jnp.einsum('bhij,bhje->bhie', s, vc)
        cross = jnp.einsum('bhid,bhde->bhie', qc, R) * decay_q
        R = R * decay_c + jnp.einsum('bhjd,bhje->bhde', kc * decay_k, vc)
        return R, inner + cross

    R, out = lax.scan(step, state0, (chunks(q), chunks(k), chunks(v)))
    out = out.transpose(1, 2, 0, 3, 4).reshape(B, H, L, -1)
    return out, R


def retention_final_state(k, v, log_gamma, reverse):
    L = k.shape[2]
    pos = jnp.arange(L, dtype=jnp.float32)
    expo = pos if reverse else (L - 1.0 - pos)
    w = jnp.exp(log_gamma[:, None] * expo)[None, :, :, None]
    return jnp.einsum('bhld,bhle->bhde', k.astype(jnp.float32) * w, v.astype(jnp.float32))


def retention_branch(q, k, v, g, log_gamma, gn_w, s_fwd, s_bwd):
    out_f, fin_f = retention_chunked(q, k, v, log_gamma[0], s_fwd, strict=False)
    out_b, fin_b = retention_chunked(jnp.flip(q, 2), jnp.flip(k, 2), jnp.flip(v, 2),
                                     log_gamma[1], s_bwd, strict=True)
    y = out_f + jnp.flip(out_b, 2)
    mu = jnp.mean(y, axis=-1, keepdims=True)
    var = jnp.mean(jnp.square(y - mu), axis=-1, keepdims=True)
    y = (y - mu) * lax.rsqrt(var + EPS)
    y = merge_heads(y).astype(g.dtype) * gn_w
    return y * jax.nn.silu(g), fin_f, fin_b


def sink_softmax(logits, sink):
    m = jnp.maximum(jnp.max(logits, axis=-1, keepdims=True), sink)
    p = jnp.exp(logits - m)
    return p / (jnp.sum(p, axis=-1, keepdims=True) + jnp.exp(sink - m))


def band_blocks(t):
    B, K, L, d = t.shape
    nb = L // ATT_BLOCK
    tp = jnp.pad(t, ((0, 0), (0, 0), (ATT_BLOCK, ATT_BLOCK), (0, 0))).reshape(B, K, nb + 2, ATT_BLOCK, d)
    return jnp.concatenate([tp[:, :, 0:nb], tp[:, :, 1:nb + 1], tp[:, :, 2:nb + 2]], axis=3)


def windowed_gqa(q, k, v, k_ctx, v_ctx, sink):
    B, Hq, L, dh = q.shape
    Hkv = k.shape[1]
    G = Hq // Hkv
    nb = L // ATT_BLOCK
    Lc = k_ctx.shape[2]
    qb = q.astype(jnp.float32).reshape(B, Hkv, G, nb, ATT_BLOCK, dh) * (dh ** -0.5)
    kb = band_blocks(k.astype(jnp.float32))
    vb = band_blocks(v.astype(jnp.float32))
    qpos = jnp.arange(nb)[:, None, None] * ATT_BLOCK + jnp.arange(ATT_BLOCK)[None, :, None]
    kpos = jnp.arange(nb)[:, None, None] * ATT_BLOCK - ATT_BLOCK + jnp.arange(3 * ATT_BLOCK)[None, None, :]
    valid = (jnp.abs(qpos - kpos) <= WINDOW) & (kpos >= 0) & (kpos < L)
    s_loc = jnp.where(valid, jnp.einsum('bkgnqd,bknjd->bkgnqj', qb, kb), NEG)
    s_ctx = jnp.einsum('bkgnqd,bkcd->bkgnqc', qb, k_ctx.astype(jnp.float32))
    p = sink_softmax(jnp.concatenate([s_ctx, s_loc], axis=-1),
                     sink.astype(jnp.float32).reshape(1, Hkv, G, 1, 1, 1))
    out = (jnp.einsum('bkgnqc,bkcd->bkgnqd', p[..., :Lc], v_ctx.astype(jnp.float32))
           + jnp.einsum('bkgnqj,bknjd->bkgnqd', p[..., Lc:], vb))
    return out.reshape(B, Hq, L, dh).astype(q.dtype)


def context_gqa(q, k, v, sink):
    B, Hq, Lc, dh = q.shape
    Hkv = k.shape[1]
    G = Hq // Hkv
    qg = q.astype(jnp.float32).reshape(B, Hkv, G, Lc, dh) * (dh ** -0.5)
    s = jnp.einsum('bkgqd,bkcd->bkgqc', qg, k.astype(jnp.float32))
    p = sink_softmax(s, sink.astype(jnp.float32).reshape(1, Hkv, G, 1, 1))
    out = jnp.einsum('bkgqc,bkcd->bkgqd', p, v.astype(jnp.float32))
    return out.reshape(B, Hq, Lc, dh).astype(q.dtype)


def setup_inputs(seed: int = 0) -> dict:
    key = jax.random.key(seed)
    ks = jax.random.split(key, 12)
    f32 = jnp.float32
    x = jax.random.normal(ks[0], (BATCH, SEQ, D_MODEL), f32)
    c = jax.random.normal(ks[1], (BATCH, D_MODEL), f32)
    ctx = jax.random.normal(ks[2], (BATCH, CTX_LEN, D_MODEL), f32)
    c_ctx = jax.random.normal(ks[3], (D_MODEL,), f32)
    w_ada = jax.random.normal(ks[4], (DEPTH, D_MODEL, 3 * D_MODEL), f32) * (0.5 * D_MODEL ** -0.5)
    b_ada = jax.random.normal(ks[5], (DEPTH, 3 * D_MODEL), f32) * 0.02
    w_in = jax.random.normal(ks[6], (DEPTH, D_MODEL, IN_COLS), f32) * (D_MODEL ** -0.5)
    base_logit = jnp.log(2.0 ** (5.0 + jnp.arange(RET_HEADS, dtype=f32)) - 1.0)
    ret_decay_logit = base_logit[None, None, :] + 0.1 * jax.random.normal(ks[7], (DEPTH, 2, RET_HEADS), f32)
    ret_gn_w = 1.0 + 0.02 * jax.random.normal(ks[8], (DEPTH, RET_W), f32)
    att_sink = 0.5 * jax.random.normal(ks[9], (DEPTH, ATT_HEADS), f32)
    w_out = jax.random.normal(ks[10], (DEPTH, D_MIX, D_MODEL), f32) * (D_MIX ** -0.5)
    final_norm_w = 1.0 + 0.02 * jax.random.normal(ks[11], (D_MODEL,), f32)
    return {"x": x, "c": c, "ctx": ctx, "c_ctx": c_ctx, "w_ada": w_ada, "b_ada": b_ada,
            "w_in": w_in, "ret_decay_logit": ret_decay_logit, "ret_gn_w": ret_gn_w,
            "att_sink": att_sink, "w_out": w_out, "final_norm_w": final_norm_w}


def reference(x, c, ctx, c_ctx, w_ada, b_ada, w_in, ret_decay_logit, ret_gn_w, att_sink, w_out, final_norm_w):
    B, L, _ = x.shape
    ROWS = L // GRID_W
    rows = jnp.broadcast_to(jnp.arange(ROWS, dtype=jnp.int32)[:, None], (ROWS, GRID_W)).reshape(-1)
    cols = jnp.broadcast_to(jnp.arange(GRID_W, dtype=jnp.int32)[None, :], (ROWS, GRID_W)).reshape(-1)

    for l in range(DEPTH):
        shift, scale, gate = jnp.split(jax.nn.silu(c) @ w_ada[l] + b_ada[l], 3, axis=-1)
        shift_c, scale_c, gate_c = jnp.split(jax.nn.silu(c_ctx) @ w_ada[l] + b_ada[l], 3, axis=-1)
        h = rms_norm(x) * (1.0 + scale[:, None, :]) + shift[:, None, :]
        hc = rms_norm(ctx) * (1.0 + scale_c) + shift_c

        rq, rk, rv, rg, aq, ak, av, ag = project(h, w_in[l])
        crq, crk, crv, crg, caq, cak, cav, cag = project(hc, w_in[l])
        rq, rk = axial_rope(rq, rows, cols), axial_rope(rk, rows, cols)
        aq, ak = axial_rope(aq, rows, cols), axial_rope(ak, rows, cols)
        log_gamma = jax.nn.log_sigmoid(ret_decay_logit[l].astype(jnp.float32))

        if l < DEPTH - 1:
            zero = jnp.zeros((B, RET_HEADS, RET_DK, RET_DV), jnp.float32)
            ret_c, s_fwd, s_bwd = retention_branch(crq, crk, crv, crg, log_gamma, ret_gn_w[l], zero, zero)
            att_c = merge_heads(context_gqa(caq, cak, cav, att_sink[l])) * jax.nn.silu(cag)
            ctx_next = ctx + gate_c * (jnp.concatenate([ret_c, att_c], axis=-1) @ w_out[l])
        else:
            s_fwd = retention_final_state(crk, crv, log_gamma[0], reverse=False)
            s_bwd = retention_final_state(crk, crv, log_gamma[1], reverse=True)
            ctx_next = ctx

        ret_x, _, _ = retention_branch(rq, rk, rv, rg, log_gamma, ret_gn_w[l], s_fwd, s_bwd)
        att_x = merge_heads(windowed_gqa(aq, ak, av, cak, cav, att_sink[l])) * jax.nn.silu(ag)
        mixed = jnp.concatenate([ret_x, att_x], axis=-1) @ w_out[l]
        x = x + gate[:, None, :] * mixed
        ctx = ctx_next

    return rms_norm(x) * final_norm_w
```

```python
from contextlib import ExitStack
import numpy as np
import concourse.bass as bass
import concourse.mybir as mybir
from concourse.bass_utils import run_bass_kernel_spmd

F32 = mybir.dt.float32
BF16 = mybir.dt.bfloat16
F32R = mybir.dt.float32r
ALU = mybir.AluOpType
AF = mybir.ActivationFunctionType

D = 1024
T = 4096
LC = 256
NT = 2
SB = NT * 128
NSB = T // SB
NBLK = T // 128
RQ = 5
RK = 6
EPS = 1e-6
SDK = 128 ** -0.5
ROPE_BASE = 10000.0


class Prog:
    EPOCH = 12000
    NDMA = 24
    PSUM = frozenset(['pj0', 'pj1', 'pj2', 'pm', 'c0', 'c1', 'c2', 'c3'])

    def __init__(self, nc):
        self.nc = nc
        self.eng = dict(pe=nc.tensor, act=nc.scalar, dve=nc.vector,
                        pool=nc.gpsimd, sp=nc.sync)
        self.ins = []
        self.last_w = {}
        self.readers = {}
        self.tail = {}
        self.pend_dma = []
        self.cur = None
        self.filler = None
        self.filler_from = 0
        self.nfill = 0
        self.tfin = []
        self.tstart = []
        self.crit = []
        self.efree = {}
        self.done_marks = set()

    import os as _os
    _PEF = float(_os.environ.get('PEF', '1900'))
    COST = dict(pe=(0.03, 1 / _PEF), act=(0.19, 1 / 1400.), dve=(0.16, 1 / 960.), pool=(0.1, 0.0023), sp=(0.05, 0.0))
    LAT = 0.28
    FILL_MIN = 0.08
    FILL_FLOOR = 3
    FILL_MAX = 64
    FILL_FRAC = 1.4

    def _deps(self, engine, reads, writes, extra):
        deps = set(extra)
        pr = [t for t in reads if t in self.PSUM]
        if pr:
            reads = [t for t in reads if t not in self.PSUM]
            writes = list(writes) + [t for t in pr if t not in writes]
        for t in reads:
            w = self.last_w.get(t)
            if w is not None:
                deps.add(w)
        for t in writes:
            w = self.last_w.get(t)
            if w is not None:
                deps.add(w)
            for r in self.readers.get(t, {}).values():
                deps.update(r)
        return deps, reads, writes

    def _start(self, engine, deps):
        t0 = self.efree.get(engine, 0.0)
        for d in deps:
            t = self.tfin[d] + (0.02 if self.ins[d]['e'] == engine else self.LAT)
            if t > t0:
                t0 = t
        return t0

    def op(self, engine, fn, reads=(), writes=(), dma=False, extra=(), n=256, needs=(), marks=()):
        if self.cur is not None:
            self.cur.append(dict(engine=engine, fn=fn, reads=reads, writes=writes, dma=dma, extra=extra, n=n,
                                 needs=needs, marks=marks))
            return None
        idx = len(self.ins)
        deps, reads, writes = self._deps(engine, reads, writes, extra)
        t0 = self._start(engine, deps)
        efree0 = self.efree.get(engine, 0.0)
        crit = ('eng', self.tail.get(engine) if not dma else None)
        for d in deps:
            if self.tfin[d] + (0.02 if self.ins[d]['e'] == engine else self.LAT) >= t0 - 1e-9:
                crit = ('dep', d)
        self.tstart.append(t0)
        self.crit.append(crit)
        if dma:
            self.tfin.append(t0 + 2.0 + n * 0.004)
            self.efree[engine] = t0 + 0.05
        else:
            a, b = self.COST[engine]
            self.tfin.append(t0 + a + b * n)
            self.efree[engine] = self.tfin[-1]
        self.done_marks.update(marks)
        self.ins.append(dict(e=engine, fn=fn, deps=deps, dma=dma, stall=t0 - efree0))
        for t in reads:
            rd = self.readers.setdefault(t, {})
            if dma:
                rd.setdefault('dma', []).append(idx)
            else:
                rd[engine] = [idx]
        for t in writes:
            self.last_w[t] = idx
            self.readers[t] = {}
        if dma:
            self.pend_dma.append(idx)
        else:
            self.tail[engine] = idx
        return idx

    def begin(self):
        self.cur = []

    def end(self):
        c, self.cur = self.cur, None
        return c

    def merge(self, *streams):
        heads = [0] * len(streams)
        while True:
            best = None
            rem = False
            for si, S in enumerate(streams):
                if heads[si] >= len(S):
                    continue
                rem = True
                dsc = S[heads[si]]
                if any(m not in self.done_marks for m in dsc['needs']):
                    continue
                deps, _, _ = self._deps(dsc['engine'], dsc['reads'], dsc['writes'], dsc['extra'])
                t0 = self._start(dsc['engine'], deps)
                t0 -= 2.5 * (len(S) - heads[si]) / max(1, len(S))
                if best is None or t0 < best[0]:
                    best = (t0, si)
            if not rem:
                break
            assert best is not None, "merge deadlock: unmet stream markers"
            si = best[1]
            dsc = streams[si][heads[si]]
            heads[si] += 1
            self.op(dsc['engine'], dsc['fn'], reads=dsc['reads'], writes=dsc['writes'], dma=dsc['dma'], extra=dsc['extra'],
                    n=dsc['n'], needs=(), marks=dsc['marks'])

    def barrier(self):
        tails = list(self.tail.values()) + list(self.pend_dma)
        self.pend_dma = []
        for e in ('pe', 'act', 'dve', 'pool', 'sp'):
            self.op(e, lambda eng: eng.nop(), extra=tails)

    def emit(self, stack):
        nc = self.nc
        ins = self.ins
        n = len(ins)
        has_dep = [False] * n
        for I in ins:
            for d in I['deps']:
                if ins[d]['e'] == 'pe' and I['e'] == 'pe' and not ins[d]['dma']:
                    continue
                has_dep[d] = True
        cnt = dict(pe=0, act=0, dve=0, pool=0, sp=0)
        ndma = 0
        nq = {}
        for idx, I in enumerate(ins):
            if I['dma']:
                q = 'pdma' if I['e'] == 'pool' else 'dma'
                I['q'] = q
                I['k'] = nq.get(q, 0)
                nq[q] = I['k'] + 1
                ndma += 1
            elif has_dep[idx]:
                cnt[I['e']] += 1
                I['seq'] = cnt[I['e']]
        sems = {}
        for e in ('pe', 'act', 'dve', 'pool', 'sp'):
            for ep in range(cnt[e] // self.EPOCH + 1):
                sems[(e, ep)] = stack.enter_context(nc.semaphore(f"s_{e}_{ep}"))
        P = self.NDMA
        for j in range(P):
            sems[('dma', j)] = stack.enter_context(nc.semaphore(f"s_dma_{j}"))
        for j in range(min(P, nq.get('pdma', 0))):
            sems[('pdma', j)] = stack.enter_context(nc.semaphore(f"s_pdma_{j}"))
        waited = {e: {} for e in self.eng}
        for idx, I in enumerate(ins):
            e = I['e']
            eng = self.eng[e]
            need = {}
            for d in I['deps']:
                Dd = ins[d]
                if Dd['dma']:
                    key = (Dd['q'], Dd['k'] % P)
                    val = 16 * (Dd['k'] // P + 1)
                else:
                    if Dd['e'] == e and e == 'pe':
                        continue
                    s = Dd['seq']
                    ep = (s - 1) // self.EPOCH
                    key = (Dd['e'], ep)
                    val = s - ep * self.EPOCH
                if need.get(key, 0) < val:
                    need[key] = val
            if I['dma'] and I['k'] >= P:
                key = (I['q'], I['k'] % P)
                val = 16 * (I['k'] // P)
                if need.get(key, 0) < val:
                    need[key] = val
            towait = [(key, val) for key, val in need.items() if waited[e].get(key, 0) < val]
            if e == 'pe' and self.filler is not None and idx > self.filler_from and towait:
                nf = self.FILL_FLOOR
                if I['stall'] > self.FILL_MIN:
                    nf = max(nf, min(self.FILL_MAX, int(self.FILL_FRAC * I['stall'] / 0.055)))
                for _ in range(nf):
                    self.filler(eng)
                    self.nfill += 1
            for key, val in towait:
                eng.wait_ge(sems[key], val)
                waited[e][key] = val
            bi = I['fn'](eng)
            if I['dma']:
                bi.then_inc(sems[(I['q'], I['k'] % P)], 16)
            elif has_dep[idx]:
                s = I['seq']
                ep = (s - 1) // self.EPOCH
                bi.then_inc(sems[(e, ep)], 1)
        return cnt, ndma


def rope_tables(kind):
    t = np.arange(T)
    row = (t // 64).astype(np.float64)
    col = (t % 64).astype(np.float64)
    p = np.arange(128)
    if kind == 'ret':
        half = 64
        q = p
    else:
        half = 32
        q = p % 64
    nf = half // 2
    fi = q % half
    inv = ROPE_BASE ** (-np.arange(nf, dtype=np.float64) / nf)
    ang = np.where((fi < nf)[:, None], row[None, :] * inv[np.minimum(fi, nf - 1)][:, None],
                   col[None, :] * inv[np.maximum(fi - nf, 0)][:, None])
    sign = np.where(q < half, -1.0, 1.0)[:, None]
    return np.cos(ang).astype(np.float32), (sign * np.sin(ang)).astype(np.float32)


def perm_matrix(kind):
    Pm = np.zeros((128, 128), np.float32)
    for m in range(128):
        if kind == 'ret':
            s = (m + 64) % 128
        else:
            b = (m // 64) * 64
            s = b + ((m - b) + 32) % 64
        Pm[s, m] = 1
    return Pm


_CONST = {}


def host_consts():
    if _CONST:
        return _CONST
    jj = np.arange(128)
    c = _CONST
    c['ident'] = np.eye(128, dtype=np.float32)
    c['perm'] = np.concatenate([perm_matrix('ret'), perm_matrix('att')], axis=1)
    Mf = np.maximum(jj[None, :] - jj[:, None], 0).astype(np.float32)
    Mb = np.maximum(jj[:, None] - jj[None, :], 0).astype(np.float32)
    c['mfb'] = np.concatenate([Mf, Mb], axis=1)
    io1 = np.tile((jj + 1).astype(np.float32)[None, :], (128, 1))
    io2 = np.tile((128 - jj).astype(np.float32)[None, :], (128, 1))
    c['io12'] = np.concatenate([io1, io2], axis=1)
    c['jvec'] = np.stack([127 - jj, jj, 255 - jj, 128 + jj], axis=1).astype(np.float32)
    mprev = np.where(jj[:, None] >= jj[None, :], 0.0, -30000.0).astype(np.float32)
    mnext = np.where(jj[:, None] <= jj[None, :], 0.0, -30000.0).astype(np.float32)
    c['mask'] = np.concatenate([mprev, mnext], axis=1)
    cm = (np.eye(128) - 1.0 / 128).astype(np.float32)
    om = np.full((128, 128), 1.0 / 128, np.float32)
    c['cmo'] = np.concatenate([cm, om], axis=1)
    csr, ssr = rope_tables('ret')
    csa, ssa = rope_tables('att')
    tab = np.stack([csr, ssr, csa, ssa], axis=1)
    tab = tab.reshape(128, 4, NSB, SB).transpose(0, 2, 1, 3)
    c['rtab'] = np.ascontiguousarray(tab.reshape(128, NSB * 4 * SB))
    return c


DRAM_INPUTS = [
    ("x", [T, D]), ("ctx", [LC, D]), ("cT", [128, 16]), ("w_ada", [D, 3 * D]),
    ("b_adaT", [128, 24]), ("b_gate", [1, D]), ("w_in", [D, 3328]), ("w_out", [D, D]),
    ("rdl", [1, 8]), ("gnwT", [128, 4]), ("sink", [1, 8]), ("fnw", [1, D]),
    ("ident", [128, 128]), ("perm", [128, 256]), ("mfb", [128, 256]), ("io12", [128, 256]),
    ("jvec", [128, 4]), ("mask", [128, 256]), ("cmo", [128, 256]),
    ("rtab", [128, NSB * 4 * SB]),
]


def build(taps=()):
    nc = bass.Bass("TRN2", target_bir_lowering=False)
    dr = {}
    for nme, shp in DRAM_INPUTS:
        dr[nme] = nc.dram_tensor(nme, shp, F32, kind="ExternalInput").ap()
    out = nc.dram_tensor("out", [T, D], F32, kind="ExternalOutput").ap()
    tap_out = {}
    st = ExitStack()
    with st:
        p = Prog(nc)

        def sb(name, shape, dt):
            return st.enter_context(nc.sbuf_tensor('s_' + name, shape, dt))

        def ps(name, shape, dt):
            return st.enter_context(nc.psum_tensor('p_' + name, shape, dt))

        def bcl(a, n):
            return bass.AP(a.tensor, a.offset, [list(z) for z in a.ap] + [[0, n]])

        pj = [ps("pj0", [128, 512], F32), ps("pj1", [128, 512], F32), ps("pj2", [128, 512], F32)]
        pm = ps("pm", [128, 512], F32)
        tp = pm[:, :].bitcast(BF16)
        cb = [ps(f"c{i}", [128, 512], F32) for i in range(4)]
        sc = [cb[0], cb[1]]

        Wb = sb("Wb", [128, 8, 3328], BF16)
        Wo = sb("Wo", [128, 8, 1024], BF16)
        RB = sb("RB", [128, NBLK, 4, 128], BF16)
        identb = sb("identb", [128, 128], BF16)
        permb = sb("permb", [128, 256], BF16)
        cmo = sb("cmo_r", [128, 256], F32R)
        DT = sb("DT", [128, 4, 128], F32)
        dq = sb("dq", [128, 2, 4, 128], BF16)
        maskb = sb("maskb", [128, 2, 4, 128], BF16)
        fnw_bc = sb("fnw_bc", [128, 1024], F32)
        modv = sb("modv", [128, 16, 2], F32)
        lg = sb("lg", [128, 8], F32)
        cdec = sb("cdec", [128, 8], F32)
        wk = sb("wk", [128, 4, 4], F32)
        gnw = sb("gnw", [128, 4], F32)
        esr = sb("esr", [1, 2, 512], BF16)
        sel = sb("sel", [1, 2, 128], BF16)
        cakT = sb("cakT", [128, 256], BF16)
        cav = sb("cav", [128, 2, 2, 128], BF16)
        Rf = sb("Rf", [128, 4, 128], F32)
        xf = sb("xf", [128, 2, 1024], F32)
        ssq = sb("ssq", [128, 40], F32)
        rstd = sb("rstd", [128, 40], F32)
        ssq2 = sb("ssq2", [128, NBLK], F32)
        r2 = sb("r2", [128, NBLK], F32)
        epst = sb("epst", [128, 3], F32)

        ph0 = ExitStack()
        with ph0:
            def sb0(name, shape, dt):
                return ph0.enter_context(nc.sbuf_tensor('z_' + name, shape, dt))
            onesf = sb0("onesf", [128, 128], F32)
            wst = [sb0(f"wst{i}", [128, 3328], F32) for i in range(4)]
            cT = sb0("cT", [128, 16], F32)
            scv = sb0("scv", [128, 16], F32)
            crep = sb0("crep", [128, 8, 128], F32)
            b_adaT = sb0("b_adaT", [128, 24], F32)
            bgate = sb0("bgate", [128, 1024], F32)
            gate_bc = xf[:, 1, :]
            rdl = sb0("rdl", [128, 8], F32)
            sink_sb = sb0("sink_sb", [128, 8], F32)
            esk = sb0("esk", [128, 8], F32)
            tmpc = sb0("tmpc", [128, 256], F32)
            mfb = sb0("mfb", [128, 256], F32)
            io12 = sb0("io12", [128, 256], F32)
            jvec = sb0("jvec", [128, 4], F32)
            tmp1 = sb0("tmp1", [128, 128], F32)
            tmp2 = sb0("tmp2", [128, 128], F32)
            wkt = sb0("wkt", [128, 4, 4], F32)
            small = sb0("small", [128, 8], F32)

            def ld(dst, src, name):
                p.op('sp', lambda e: e.dma_start(out=dst, in_=src), writes=[name], dma=True)

            ld(cT[:], dr['cT'][:, :], 'cT')
            ld(b_adaT[:], dr['b_adaT'][:, :], 'b_adaT')
            ld(rdl[:], dr['rdl'].partition_broadcast(128), 'rdl')
            ld(sink_sb[:], dr['sink'].partition_broadcast(128), 'sink_sb')
            ld(gnw[:], dr['gnwT'][:, :], 'gnw')
            ld(jvec[:], dr['jvec'][:, :], 'jvec')
            p.op('dve', lambda e: e.memset(onesf[:], 1.0), writes=['onesf'])
            p.op('dve', lambda e: e.memset(epst[:, 0:1], EPS), writes=['epst'])
            p.op('dve', lambda e: e.memset(epst[:, 1:2], D * EPS), writes=['epst'])
            p.op('dve', lambda e: e.memset(epst[:, 2:3], 1.0), writes=['epst'])
            p.op('act', lambda e: e.activation(out=scv[:], in_=cT[:], func=AF.Silu), reads=['cT'], writes=['scv'])
            for kc in range(8):
                p.op('dve', (lambda kc: lambda e: e.tensor_scalar(out=crep[:, kc, :], in0=onesf[:], scalar1=scv[:, kc:kc + 1],
                                                                 scalar2=None, op0=ALU.mult))(kc),
                     reads=['onesf', 'scv'], writes=['crep'])
            for kc in range(8):
                w = wst[kc % 4]
                wn = f'wst{kc % 4}'
                ld(w[:, 0:3072], dr['w_ada'][kc * 128:(kc + 1) * 128, :], wn)
                for j in range(16):
                    p.op('pe', (lambda w, j, kc: lambda e: e.matmul(pj[0][:, 2 * j:2 * j + 2], lhsT=w[:, j * 128:(j + 1) * 128],
                                                                   rhs=scv[:, kc::8], start=(kc == 0 and j == 0), stop=(kc == 7 and j == 15)))(w, j, kc),
                         reads=[wn, 'scv'], writes=['pj0'])
                for nn in range(2):
                    p.op('pe', (lambda w, nn, kc: lambda e: e.matmul(sc[nn][:, :], lhsT=crep[:, kc, :],
                                                                    rhs=w[:, 2048 + nn * 512:2048 + (nn + 1) * 512],
                                                                    start=(kc == 0), stop=(kc == 7)))(w, nn, kc),
                         reads=[wn, 'crep'], writes=[f'c{nn}'])
            p.op('dve', lambda e: e.tensor_tensor(out=modv[:, :, :], in0=pj[0][:, 0:32].rearrange("p (j t) -> p j t", t=2),
                                                  in1=bcl(b_adaT[:, 0:16], 2), op=ALU.add),
                 reads=['pj0', 'b_adaT'], writes=['modv'])
            p.op('dve', lambda e: e.tensor_scalar(out=modv[:, 8:16, :], in0=modv[:, 8:16, :], scalar1=1.0, scalar2=None, op0=ALU.add),
                 reads=['modv'], writes=['modv'])
            ld(bgate[:], dr['b_gate'].partition_broadcast(128), 'bgate')
            ld(fnw_bc[:], dr['fnw'].partition_broadcast(128), 'fnw_bc')
            for nn in range(2):
                p.op('dve', (lambda nn: lambda e: e.tensor_tensor(out=xf[:, 1, nn * 512:(nn + 1) * 512], in0=sc[nn][:, :],
                                                                 in1=bgate[:, nn * 512:(nn + 1) * 512], op=ALU.add))(nn),
                     reads=[f'c{nn}', 'bgate'], writes=['xf1'])
            p.op('dve', lambda e: e.tensor_scalar(out=fnw_bc[:], in0=fnw_bc[:], scalar1=32.0, scalar2=None, op0=ALU.mult),
                 reads=['fnw_bc'], writes=['fnw_bc'])
            p.op('act', lambda e: e.activation(out=small[:], in_=rdl[:], func=AF.Exp, scale=-1.0), reads=['rdl'], writes=['small'])
            p.op('dve', lambda e: e.tensor_scalar(out=small[:], in0=small[:], scalar1=1.0, scalar2=None, op0=ALU.add),
                 reads=['small'], writes=['small'])
            p.op('act', lambda e: e.activation(out=lg[:], in_=small[:], func=AF.Ln), reads=['small'], writes=['lg'])
            p.op('dve', lambda e: e.tensor_scalar(out=lg[:], in0=lg[:], scalar1=-1.0, scalar2=None, op0=ALU.mult),
                 reads=['lg'], writes=['lg'])
            p.op('act', lambda e: e.activation(out=cdec[:], in_=lg[:], func=AF.Exp, scale=128.0), reads=['lg'], writes=['cdec'])
            p.op('act', lambda e: e.activation(out=esk[:], in_=sink_sb[:], func=AF.Exp), reads=['sink_sb'], writes=['esk'])
            ld(mfb[:], dr['mfb'][:, :], 'mfb')
            ld(io12[:], dr['io12'][:, :], 'io12')
            for h in range(4):
                p.op('dve', (lambda h: lambda e: e.tensor_scalar(out=tmp1[:], in0=mfb[:, 0:128], scalar1=lg[:, h:h + 1], scalar2=None,
                                                                op0=ALU.mult))(h), reads=['mfb', 'lg'], writes=['tmp1'])
                p.op('dve', (lambda h: lambda e: e.scalar_tensor_tensor(out=tmp2[:], in0=mfb[:, 128:256], scalar=lg[:, 4 + h:5 + h],
                                                                       in1=tmp1[:], op0=ALU.mult, op1=ALU.add))(h),
                     reads=['mfb', 'lg', 'tmp1'], writes=['tmp2'])
                p.op('act', (lambda h: lambda e: e.activation(out=DT[:, h, :], in_=tmp2[:], func=AF.Exp))(h),
                     reads=['tmp2'], writes=['DT'])
                p.op('act', (lambda h: lambda e: e.activation(out=dq[:, 0, h, :], in_=io12[:, 0:128], func=AF.Exp,
                                                             scale=lg[:, h:h + 1]))(h), reads=['io12', 'lg'], writes=['dq'])
                p.op('act', (lambda h: lambda e: e.activation(out=dq[:, 1, h, :], in_=io12[:, 128:256], func=AF.Exp,
                                                             scale=lg[:, 4 + h:5 + h]))(h), reads=['io12', 'lg'], writes=['dq'])
                for kind in range(4):
                    col = (kind % 2) * 4 + h
                    p.op('dve', (lambda h, kind, col: lambda e: e.tensor_scalar(out=wkt[:, kind, h:h + 1], in0=jvec[:, kind:kind + 1],
                                                                               scalar1=lg[:, col:col + 1], scalar2=None,
                                                                               op0=ALU.mult))(h, kind, col),
                         reads=['jvec', 'lg'], writes=['wkt'])
            p.op('dve', lambda e: e.tensor_scalar(out=DT[:], in0=DT[:], scalar1=SDK, scalar2=None, op0=ALU.mult),
                 reads=['DT'], writes=['DT'])
            p.op('act', lambda e: e.activation(out=wk[:], in_=wkt[:], func=AF.Exp), reads=['wkt'], writes=['wk'])
            p.op('dve', lambda e: e.tensor_scalar(out=wk[:], in0=wk[:], scalar1=SDK, scalar2=None, op0=ALU.mult),
                 reads=['wk'], writes=['wk'])
            for a in range(2):
                for g in range(4):
                    c0 = a * 4 + g
                    p.op('dve', (lambda a, g, c0: lambda e: e.tensor_scalar(out=esr[0:1, a, g * 128:(g + 1) * 128], in0=onesf[0:1, :],
                                                                           scalar1=esk[0:1, c0:c0 + 1], scalar2=None,
                                                                           op0=ALU.mult))(a, g, c0),
                         reads=['onesf', 'esk'], writes=['esr'])
            p.op('dve', lambda e: e.memset(sel[:], 0.0), writes=['sel'])
            p.op('dve', lambda e: e.memset(sel[0:1, 0, 64:128], 1.0), writes=['sel'])
            p.op('dve', lambda e: e.memset(sel[0:1, 1, 0:64], 1.0), writes=['sel'])
            ld(tmpc[:, 0:128], dr['ident'][:, :], 'tmpc')
            p.op('dve', lambda e: e.tensor_copy(out=identb[:], in_=tmpc[:, 0:128]), reads=['tmpc'], writes=['identb'])
            ld(tmpc[:], dr['perm'][:, :], 'tmpc')
            p.op('dve', lambda e: e.tensor_copy(out=permb[:], in_=tmpc[:]), reads=['tmpc'], writes=['permb'])
            ld(tmpc[:], dr['cmo'][:, :], 'tmpc')
            p.op('dve', lambda e: e.tensor_copy(out=cmo[:], in_=tmpc[:]), reads=['tmpc'], writes=['cmo'])
            ld(tmpc[:], dr['mask'][:, :], 'tmpc')
            for g in range(4):
                p.op('dve', (lambda g: lambda e: e.tensor_copy(out=maskb[:, :, g, :],
                                                              in_=tmpc[:].rearrange("p (m i) -> p m i", m=2)))(g),
                     reads=['tmpc'], writes=['maskb'])
            for kc in range(8):
                w = wst[kc % 4]
                wn = f'wst{kc % 4}'
                ld(w[:, 0:1024], dr['w_in'][kc * 128:(kc + 1) * 128, 512:1536], wn)
                ld(w[:, 1024:1280], dr['w_in'][kc * 128:(kc + 1) * 128, 2560:2816], wn + 'b')
                p.op('dve', (lambda w, kc: lambda e: e.tensor_copy(out=Wb[:, kc, 512:1536], in_=w[:, 0:1024]))(w, kc),
                     reads=[wn], writes=['Wb', wn])
                p.op('act', (lambda w, kc: lambda e: e.copy(out=Wb[:, kc, 2560:2816], in_=w[:, 1024:1280]))(w, kc),
                     reads=[wn + 'b'], writes=['Wb', wn + 'b'])
            p.barrier()
            p.filler_from = len(p.ins)
            p.filler = lambda e: e.matmul(pj[2][:, 0:64], lhsT=identb[:], rhs=identb[:, 0:64], start=True, stop=True)

        xa = sb("xa", [128, 3, 1024], F32)
        xs = sb("xs", [128, NT, 1024], BF16)
        junk = sb("junk", [128, 1024], BF16)
        hT = sb("hT", [128, 2, 8, SB], BF16)
        tabs = sb("tabs", [128, 4, SB], F32)
        Xbf = sb("Xbf", [128, SB], BF16)
        t1 = sb("t1", [128, SB], F32)
        t2 = sb("t2", [128, SB], F32)
        rqT = sb("rqT", [128, 2, 4, SB], BF16)
        rkT = sb("rkT", [128, 2, 4, SB], BF16)
        rv = sb("rv", [128, 2, NT, 512], BF16)
        sgr = sb("sgr", [128, 2, 4, SB], BF16)
        qfb = sb("qfb", [128, 2, 4, 128], BF16)
        aqT = sb("aqT", [128, RQ, 4, 128], BF16)
        sag = sb("sag", [128, RQ, 4, 128], BF16)
        akT = sb("akT", [128, RK, 128], BF16)
        avg = sb("avg", [128, RK, 2, 128], BF16)
        PT = [sb(f"PT{i}", [128, 512], BF16) for i in range(2)]
        ktm = sb("ktm", [128, 512], BF16)
        vsc = sb("vsc", [128, 512], BF16)
        RFs = sb("RFs", [128, 4, 128], BF16)
        PTr = sb("PTr", [128, 4, 128], BF16)
        ysb = sb("ysb", [128, 512], F32R)
        gnt = sb("gnt", [128, 512], F32)
        Rb = gnt[:, :].rearrange("p (h d) -> p h d", h=4)
        rdn = junk[:, :].bitcast(F32)
        retx = sb("retx", [128, 3, 4, 128], BF16)
        attx = sb("attx", [128, 2, 4, 128], BF16)

        CMm = cmo[:, 0:128]
        OMm = cmo[:, 128:256]
        c0b = cb[0][:, :].bitcast(BF16)
        c2b = cb[2][:, :].bitcast(BF16)

        def load_x(src_ap, slot):
            p.op('sp', lambda e: e.dma_start(out=xa[:, slot, :], in_=src_ap), writes=[f'xr{slot}'], dma=True)

        def load_tabs(s):
            p.op('sp', lambda e: e.dma_start(out=tabs[:, :, :].rearrange("p a t -> p (a t)"),
                                             in_=dr['rtab'][:, s * 4 * SB:(s + 1) * 4 * SB]), writes=['tabs'], dma=True)

        def xphase(hb, slots, scols, ntile, mcol, tpv, tpn):
            for i in range(ntile):
                sl, cc = slots[i], scols[i]
                p.op('act', (lambda sl, cc, i: lambda e: e.activation(out=xs[:, i, :], in_=xa[:, sl, :], func=AF.Square,
                                                                     accum_out=ssq[:, cc:cc + 1]))(sl, cc, i),
                     reads=[f'xr{sl}'], writes=[f'xs{i}', f'ssq{cc}'], n=1024)
                p.op('act', (lambda cc: lambda e: e.activation(out=rstd[:, cc:cc + 1], in_=ssq[:, cc:cc + 1], func=AF.Ln,
                                                              bias=epst[:, 1:2], scale=1.0))(cc),
                     reads=[f'ssq{cc}', 'epst'], writes=[f'rstd{cc}'])
                p.op('act', (lambda cc: lambda e: e.activation(out=rstd[:, cc:cc + 1], in_=rstd[:, cc:cc + 1], func=AF.Exp,
                                                              scale=-0.5))(cc),
                     reads=[f'rstd{cc}'], writes=[f'rstd{cc}'])
                p.op('pool', (lambda sl, cc, i: lambda e: e.tensor_scalar(out=xs[:, i, :], in0=xa[:, sl, :],
                                                                         scalar1=rstd[:, cc:cc + 1], scalar2=32.0,
                                                                         op0=ALU.mult, op1=ALU.mult))(sl, cc, i),
                     reads=[f'xr{sl}', f'rstd{cc}'], writes=[f'xs{i}'], n=1024)
            for kc in range(8):
                tslot = kc % 4
                for i in range(ntile):
                    p.op('pe', (lambda kc, i, tslot: lambda e: e.transpose(
                        out=tpv[:, tslot * 256 + i * 128:tslot * 256 + (i + 1) * 128],
                        in_=xs[:, i, kc * 128:(kc + 1) * 128], identity=identb[:]))(kc, i, tslot),
                         reads=[f'xs{i}', 'identb'], writes=[tpn], n=128)
                if kc % 2 == 0:
                    p.op('act', (lambda kc, tslot: lambda e: e.activation(
                        out=hT[:, hb, kc, 0:ntile * 128], in_=tpv[:, tslot * 256:tslot * 256 + ntile * 128], func=AF.Identity,
                        bias=modv[:, kc, mcol:mcol + 1], scale=modv[:, 8 + kc, mcol:mcol + 1]))(kc, tslot),
                         reads=[tpn, 'modv'], writes=[f'hT{hb}'])
                else:
                    p.op('dve', (lambda kc, tslot: lambda e: e.tensor_scalar(
                        out=hT[:, hb, kc, 0:ntile * 128], in0=tpv[:, tslot * 256:tslot * 256 + ntile * 128],
                        scalar1=modv[:, 8 + kc, mcol:mcol + 1], scalar2=modv[:, kc, mcol:mcol + 1],
                        op0=ALU.mult, op1=ALU.add))(kc, tslot),
                         reads=[tpn, 'modv'], writes=[f'hT{hb}'])

        pjc = [0]

        def proj_fm(col0, ncols_tok, hb=0):
            b = pjc[0] % 2
            pjc[0] += 1
            for kc in range(8):
                p.op('pe', (lambda kc, b: lambda e: e.matmul(pj[b][:, 0:ncols_tok], lhsT=Wb[:, kc, col0:col0 + 128],
                                                            rhs=hT[:, hb, kc, 0:ncols_tok], start=(kc == 0), stop=(kc == 7)))(kc, b),
                     reads=['Wb', 'WbL', f'hT{hb}'], writes=[f'pj{b}'])
            return pj[b], f'pj{b}'

        def proj_tm(i, col0, ncols, hb=0):
            b = pjc[0] % 2
            pjc[0] += 1
            for kc in range(8):
                p.op('pe', (lambda kc, b: lambda e: e.matmul(pj[b][:, 0:ncols], lhsT=hT[:, hb, kc, i * 128:(i + 1) * 128],
                                                            rhs=Wb[:, kc, col0:col0 + ncols], start=(kc == 0), stop=(kc == 7)))(kc, b),
                     reads=['Wb', 'WbL', f'hT{hb}'], writes=[f'pj{b}'])
            return pj[b], f'pj{b}'

        def rope_evac(pst, pname, kind, dst_ap, dst_names, n=SB):
            k0 = 0 if kind == 'ret' else 2
            po = 0 if kind == 'ret' else 128
            p.op('dve', lambda e: e.tensor_copy(out=Xbf[:, 0:n], in_=pst[:, 0:n]), reads=[pname], writes=['Xbf'])
            p.op('pe', lambda e: e.matmul(pm[:, 0:n], lhsT=permb[:, po:po + 128], rhs=Xbf[:, 0:n], start=True, stop=True),
                 reads=['permb', 'Xbf'], writes=['pm'])
            p.op('dve', lambda e: e.tensor_tensor(out=t1[:, 0:n], in0=pst[:, 0:n], in1=tabs[:, k0, 0:n], op=ALU.mult),
                 reads=[pname, 'tabs'], writes=['t1'])
            p.op('dve', lambda e: e.tensor_tensor(out=t2[:, 0:n], in0=pm[:, 0:n], in1=tabs[:, k0 + 1, 0:n], op=ALU.mult),
                 reads=['pm', 'tabs'], writes=['t2'])
            shp = dst_ap.shape
            if len(shp) == 3:
                a_in0 = t1[:, 0:n].rearrange("p (b t) -> p b t", b=shp[1])
                a_in1 = t2[:, 0:n].rearrange("p (b t) -> p b t", b=shp[1])
            else:
                a_in0, a_in1 = t1[:, 0:n], t2[:, 0:n]
            p.op('pool', lambda e: e.tensor_tensor(out=dst_ap, in0=a_in0, in1=a_in1, op=ALU.add),
                 reads=['t1', 't2'], writes=dst_names)

        def ktrans_U(bf, i, kind_w, ub=1):
            for h in range(4):
                p.op('pe', (lambda h: lambda e: e.transpose(out=c0b[:, h * 128:(h + 1) * 128], in_=rkT[:, bf, h, i * 128:(i + 1) * 128],
                                                           identity=identb[:]))(h),
                     reads=[f'rkT{bf}', 'identb'], writes=['c0'], n=128)
            p.op('dve', lambda e: e.tensor_copy(out=ktm[:], in_=c0b[:, 0:512]), reads=['c0'], writes=['ktm'], n=256)
            p.op('dve', lambda e: e.tensor_tensor(out=vsc[:].rearrange("p (h d) -> p h d", h=4),
                                                   in0=rv[:, bf, i, :].rearrange("p (h d) -> p h d", h=4),
                                                   in1=bcl(wk[:, kind_w, :], 128), op=ALU.mult),
                 reads=[f'rv{bf}_{i}', 'wk'], writes=['vsc'])
            for h in range(4):
                p.op('pe', (lambda h: lambda e: e.matmul(cb[ub][:, h * 128:(h + 1) * 128], lhsT=ktm[:, h * 128:(h + 1) * 128],
                                                        rhs=vsc[:, h * 128:(h + 1) * 128], start=True, stop=True))(h),
                     reads=['ktm', 'vsc'], writes=[f'c{ub}'], n=128)

        def state_update(Rstate, rname, cd0, ub=1):
            p.op('pool', lambda e: e.tensor_tensor(out=Rstate, in0=Rstate, in1=bcl(cdec[:, cd0:cd0 + 4], 128), op=ALU.mult),
                 reads=[rname, 'cdec'], writes=[rname])
            p.op('dve', lambda e: e.tensor_tensor(out=Rstate.rearrange("p h d -> p (h d)"),
                                                  in0=Rstate.rearrange("p h d -> p (h d)"), in1=cb[ub][:, :], op=ALU.add),
                 reads=[rname, f'c{ub}'], writes=[rname])

        for i in range(2):
            load_x(dr['ctx'][i * 128:(i + 1) * 128, :], i)
        xphase(0, [0, 1], [32, 33], 2, 1, tp, 'pm')
        kc_sb = [ktm, vsc]
        for i in range(2):
            pst, pn = proj_tm(i, 1024, 512)
            p.op('act', (lambda i, pst: lambda e: e.copy(out=rv[:, 0, i, :], in_=pst[:, :]))(i, pst), reads=[pn], writes=[f'rv0_{i}'])
        for dirn in range(2):
            for i in range(2):
                pst, pn = proj_tm(i, 512, 512)
                kind = (2 if i == 0 else 0) if dirn == 0 else (1 if i == 0 else 3)
                p.op('dve', (lambda i, pst, kind: lambda e: e.tensor_tensor(
                    out=kc_sb[i][:].rearrange("p (h d) -> p h d", h=4), in0=pst[:, :].rearrange("p (h d) -> p h d", h=4),
                    in1=bcl(wk[:, kind, :], 128), op=ALU.mult))(i, pst, kind), reads=[pn, 'wk'], writes=['ktm' if i == 0 else 'vsc'])
            for h in range(4):
                for i in range(2):
                    p.op('pe', (lambda h, i: lambda e: e.matmul(cb[1][:, h * 128:(h + 1) * 128], lhsT=kc_sb[i][:, h * 128:(h + 1) * 128],
                                                               rhs=rv[:, 0, i, h * 128:(h + 1) * 128], start=(i == 0), stop=(i == 1)))(h, i),
                         reads=['ktm', 'vsc', 'rv0_0', 'rv0_1'], writes=['c1'])
            Rst, rn = (Rf[:], 'Rf') if dirn == 0 else (Rb, 'gnt')
            p.op('dve', (lambda Rst: lambda e: e.tensor_copy(out=Rst.rearrange("p h d -> p (h d)"), in_=cb[1][:, :]))(Rst),
                 reads=['c1'], writes=[rn])
        pst, pn = proj_fm(2560, 256)
        p.op('act', (lambda pst: lambda e: e.copy(out=cakT[:], in_=pst[:, 0:256]))(pst), reads=[pn], writes=['cakT'])
        p.op('pool', lambda e: e.memset(cav[:], 1.0), writes=['cav'])
        p.op('pool', lambda e: e.memset(avg[:], 1.0), writes=[f'avg{m}' for m in range(RK)])
        for i in range(2):
            pst, pn = proj_tm(i, 2688, 128)
            p.op('act', (lambda i, pst: lambda e: e.copy(out=cav[:, i, 0, 0:64], in_=pst[:, 0:64]))(i, pst), reads=[pn], writes=['cav'])
            p.op('dve', (lambda i, pst: lambda e: e.tensor_copy(out=cav[:, i, 1, 64:128], in_=pst[:, 64:128]))(i, pst),
                 reads=[pn], writes=['cav'])

        class XSched:
            def __init__(self, order, par=0):
                self.order = order
                self.par = par
                self.blist = [s * NT + i for s in order for i in range(NT)]
                self.nload = 0
                self.base = 0

            def ensure(self, upto):
                while self.nload < min(upto, len(self.blist)):
                    n = self.blist[self.nload]
                    load_x(dr['x'][n * 128:(n + 1) * 128, :], (self.base + self.nload) % 3)
                    self.nload += 1

            def X(self, j, tpv=None, tpn='c0'):
                s = self.order[j]
                blocks = [s * NT + i for i in range(NT)]
                xphase((s + self.par) % 2, [(self.base + NT * j + i) % 3 for i in range(NT)], blocks, NT, 0,
                       c0b if tpv is None else tpv, tpn)
                self.ensure(NT * j + 5)

        def PA(s):
            load_tabs(s)
            bf = s % 2
            for h in range(4):
                pst, pn = proj_fm(512 + h * 128, SB, bf)
                rope_evac(pst, pn, 'ret', rkT[:, bf, h, :], [f'rkT{bf}'])
            for i in range(NT):
                pst, pn = proj_tm(i, 1024, 512, bf)
                p.op('act', (lambda i, pst: lambda e: e.copy(out=rv[:, bf, i, :], in_=pst[:, :]))(i, pst), reads=[pn], writes=[f'rv{bf}_{i}'])

        def CA(s):
            bf = s % 2
            ubank = {1: 1, 0: 3}
            for i in range(NT - 1, -1, -1):
                ktrans_U(bf, i, 1, ubank[i])
            for i in range(NT - 1, -1, -1):
                n = s * NT + i
                p.op('act', (lambda n: lambda e: e.copy(out=RB[:, n, :, :], in_=Rb))(n), reads=['gnt'], writes=[f'RB{n}'], n=512)
                state_update(Rb, 'gnt', 4, ubank[i])

        def PB(s):
            load_tabs(s)
            blocks = [s * NT + i for i in range(NT)]
            bf = (s + 1) % 2
            q0 = blocks[0] % RQ
            k0 = blocks[0] % RK
            qsl = [n % RQ for n in blocks]
            ksl = [n % RK for n in blocks]

            def c_rq(h):
                pst, pn = proj_fm(h * 128, SB, bf)
                rope_evac(pst, pn, 'ret', rqT[:, bf, h, :], [f'rqT{bf}'])

            def c_rk(h):
                pst, pn = proj_fm(512 + h * 128, SB, bf)
                rope_evac(pst, pn, 'ret', rkT[:, bf, h, :], [f'rkT{bf}'])

            def c_aq(g):
                pst, pn = proj_fm(2048 + g * 128, SB, bf)
                if qsl[1] == qsl[0] + 1:
                    rope_evac(pst, pn, 'att', aqT[:, q0:q0 + NT, g, :], [f'aqT{z}' for z in qsl])
                else:
                    rope_evac2(pst, pn, 'att', [aqT[:, z, g, :] for z in qsl], [f'aqT{z}' for z in qsl])

            def c_ak():
                pst, pn = proj_fm(2560, SB, bf)
                if ksl[1] == ksl[0] + 1:
                    rope_evac(pst, pn, 'att', akT[:, k0:k0 + NT, :], [f'akT{z}' for z in ksl])
                else:
                    rope_evac2(pst, pn, 'att', [akT[:, z, :] for z in ksl], [f'akT{z}' for z in ksl])

            def c_rg(h):
                pst, pn = proj_fm(1536 + h * 128, SB, bf)
                sigmoid_to_t1(pst, pn)
                p.op('dve', (lambda h, pst: lambda e: e.scalar_tensor_tensor(out=sgr[:, bf, h, :], in0=pst[:, 0:SB], scalar=gnw[:, h:h + 1],
                                                                            in1=t1[:, 0:SB], op0=ALU.mult, op1=ALU.mult))(h, pst),
                     reads=[pn, 'gnw', 't1'], writes=[f'sgr{bf}'])

            def c_ag(g):
                pst, pn = proj_fm(2816 + g * 128, SB, bf)
                sigmoid_to_t1(pst, pn)
                for i in range(NT):
                    p.op('dve', (lambda g, pst, i, z: lambda e: e.tensor_tensor(out=sag[:, z, g, :], in0=pst[:, i * 128:(i + 1) * 128],
                                                                               in1=t1[:, i * 128:(i + 1) * 128], op=ALU.mult))(g, pst, i, qsl[i]),
                         reads=[pn, 't1'], writes=[f'sag{qsl[i]}'], n=128)

            def c_tm(i):
                pst, pn = proj_tm(i, 1024, 512, bf)
                p.op('act', (lambda i, pst: lambda e: e.copy(out=rv[:, bf, i, :], in_=pst[:, :]))(i, pst), reads=[pn], writes=[f'rv{bf}_{i}'])
                pst, pn = proj_tm(i, 2688, 128, bf)
                ms = ksl[i]
                p.op('act', (lambda ms, pst: lambda e: e.copy(out=avg[:, ms, 0, 0:64], in_=pst[:, 0:64]))(ms, pst),
                     reads=[pn], writes=[f'avg{ms}'])
                p.op('dve', (lambda ms, pst: lambda e: e.tensor_copy(out=avg[:, ms, 1, 64:128], in_=pst[:, 64:128]))(ms, pst),
                     reads=[pn], writes=[f'avg{ms}'])

            for h in range(4):
                c_rq(h); c_rg(h)
            for h in range(4):
                c_rk(h); c_ag(h)
            for g in range(4): c_aq(g)
            c_ak()
            for i in range(NT): c_tm(i)

        def sigmoid_to_t1(pst, pname):
            p.op('act', lambda e: e.activation(out=t1[:, 0:SB], in_=pst[:, 0:SB], func=AF.Exp, scale=-1.0), reads=[pname], writes=['t1'])
            p.op('act', lambda e: e.activation(out=t1[:, 0:SB], in_=t1[:, 0:SB], func=AF.Ln, bias=epst[:, 2:3], scale=1.0),
                 reads=['t1', 'epst'], writes=['t1'])
            p.op('act', lambda e: e.activation(out=t1[:, 0:SB], in_=t1[:, 0:SB], func=AF.Exp, scale=-1.0), reads=['t1'], writes=['t1'])

        def rope_evac2(pst, pname, kind, dsts, dst_names, n=SB):
            k0 = 0 if kind == 'ret' else 2
            po = 0 if kind == 'ret' else 128
            p.op('dve', lambda e: e.tensor_copy(out=Xbf[:, 0:n], in_=pst[:, 0:n]), reads=[pname], writes=['Xbf'])
            p.op('pe', lambda e: e.matmul(pm[:, 0:n], lhsT=permb[:, po:po + 128], rhs=Xbf[:, 0:n], start=True, stop=True),
                 reads=['permb', 'Xbf'], writes=['pm'])
            p.op('dve', lambda e: e.tensor_tensor(out=t1[:, 0:n], in0=pst[:, 0:n], in1=tabs[:, k0, 0:n], op=ALU.mult),
                 reads=[pname, 'tabs'], writes=['t1'])
            p.op('dve', lambda e: e.tensor_tensor(out=t2[:, 0:n], in0=pm[:, 0:n], in1=tabs[:, k0 + 1, 0:n], op=ALU.mult),
                 reads=['pm', 'tabs'], writes=['t2'])
            for i, (d_, nm_) in enumerate(zip(dsts, dst_names)):
                p.op('pool', (lambda d_, i: lambda e: e.tensor_tensor(out=d_, in0=t1[:, i * 128:(i + 1) * 128],
                                                                     in1=t2[:, i * 128:(i + 1) * 128], op=ALU.add))(d_, i),
                     reads=['t1', 't2'], writes=[nm_])

        def retention_chunk(s, i):
            n = s * NT + i
            bf = (s + 1) % 2
            slot = n % 3
            tsl = slice(i * 128, (i + 1) * 128)
            p.op('act', lambda e: e.copy(out=RFs[:], in_=Rf[:]), reads=['Rf'], writes=['RFs'], n=512)
            ktrans_U(bf, i, 0)
            state_update(Rf[:], 'Rf', 0)
            p.op('dve', lambda e: e.tensor_tensor(out=qfb[:, 0, :, :], in0=rqT[:, bf, :, tsl], in1=dq[:, 0, :, :], op=ALU.mult),
                 reads=[f'rqT{bf}', 'dq'], writes=['qf'])
            p.op('dve', lambda e: e.tensor_tensor(out=qfb[:, 1, :, :], in0=rqT[:, bf, :, tsl], in1=dq[:, 1, :, :], op=ALU.mult),
                 reads=[f'rqT{bf}', 'dq'], writes=['qb'])
            STb, stn = cb[0], 'c0'
            Yb, yn = cb[1], 'c1'
            for h in range(4):
                p.op('pe', (lambda h: lambda e: e.matmul(STb[:, h * 128:(h + 1) * 128], lhsT=rkT[:, bf, h, tsl], rhs=rqT[:, bf, h, tsl],
                                                        start=True, stop=True))(h), reads=[f'rkT{bf}', f'rqT{bf}'], writes=[stn], n=128)
            p.op('dve', lambda e: e.tensor_tensor(out=PTr[:].rearrange("p h i -> p (h i)"), in0=STb[:, :],
                                                  in1=DT[:].rearrange("p h i -> p (h i)"), op=ALU.mult),
                 reads=[stn, 'DT'], writes=['PTr'], n=512)
            for h in range(4):
                ysl = slice(h * 128, (h + 1) * 128)
                p.op('pe', (lambda h, ysl: lambda e: e.matmul(Yb[:, ysl], lhsT=rv[:, bf, i, ysl], rhs=PTr[:, h, :], start=True, stop=False))(h, ysl),
                     reads=[f'rv{bf}_{i}', 'PTr'], writes=[yn], n=128)
                p.op('pe', (lambda h, ysl: lambda e: e.matmul(Yb[:, ysl], lhsT=RFs[:, h, :], rhs=qfb[:, 0, h, :], start=False, stop=False))(h, ysl),
                     reads=['RFs', 'qf'], writes=[yn], n=128)
                p.op('pe', (lambda h, ysl: lambda e: e.matmul(Yb[:, ysl], lhsT=RB[:, n, h, :], rhs=qfb[:, 1, h, :], start=False, stop=True))(h, ysl),
                     reads=[f'RB{n}', 'qb'], writes=[yn], n=128)
            p.op('act', lambda e: e.copy(out=ysb[:], in_=Yb[:, :]), reads=[yn], writes=['ysb'], n=512)
            p.op('pe', lambda e: e.matmul(cb[0][:, :], lhsT=CMm, rhs=ysb[:], start=True, stop=True), reads=['cmo', 'ysb'], writes=['c0'], n=512)
            p.op('act', lambda e: e.activation(out=ysb[:], in_=cb[0][:, :], func=AF.Square), reads=['c0'], writes=['ysb'], n=512)
            p.op('pe', lambda e: e.matmul(cb[1][:, :], lhsT=OMm, rhs=ysb[:], start=True, stop=True), reads=['cmo', 'ysb'], writes=['c1'], n=512)
            p.op('act', lambda e: e.activation(out=gnt[:], in_=cb[1][:, :], func=AF.Ln, bias=epst[:, 0:1], scale=1.0),
                 reads=['c1', 'epst'], writes=['gnt'], n=512)
            p.op('act', lambda e: e.activation(out=gnt[:], in_=gnt[:], func=AF.Exp, scale=-0.5), reads=['gnt'], writes=['gnt'], n=512)
            p.op('dve', lambda e: e.tensor_tensor(out=gnt[:], in0=cb[0][:, :], in1=gnt[:], op=ALU.mult), reads=['c0', 'gnt'], writes=['gnt'], n=512)
            p.op('pool', lambda e: e.tensor_tensor(out=retx[:, slot, :, :], in0=gnt[:].rearrange("p (h i) -> p h i", h=4),
                                                   in1=sgr[:, bf, :, tsl], op=ALU.mult), reads=['gnt', f'sgr{bf}'], writes=[f'retx{slot}'],
                 n=512, marks=[f'retx_done{n}'])

        ptc = [0]
        scc = [0]
        aoc = [0]

        def attention_block(n):
            load_xf(n)
            for a in range(2):
                att_kv(n, a)

        def att_kv(n, a):
            slot = n % 2
            qs = n % RQ
            pa = slice(a * 64, (a + 1) * 64)
            pd = slice((1 - a) * 64, (2 - a) * 64)
            aob_i = 3
            aob, aon = cb[aob_i], f'c{aob_i}'
            kbs = [('c', 0, None), ('c', 1, None)]
            if n - 1 >= 0:
                kbs.append(('l', n - 1, 0))
            kbs.append(('l', n, None))
            if n + 1 < NBLK:
                kbs.append(('l', n + 1, 1))
            for ki, (typ, m, msk) in enumerate(kbs):
                sb_i = scc[0] % 2
                scc[0] += 1
                scb, scn = cb[2], 'c2'
                if typ == 'c':
                    lk = cakT[pa, m * 128:(m + 1) * 128]
                    lkn = 'cakT'
                    lv = cav[:, m, a, :]
                    lvn = 'cav'
                else:
                    ms = m % RK
                    lk = akT[pa, ms, :]
                    lkn = f'akT{ms}'
                    lv = avg[:, ms, a, :]
                    lvn = f'avg{ms}'
                p.op('pe', (lambda scb, lk, msk: lambda e: e.matmul(scb[:, :], lhsT=lk, rhs=aqT[pa, qs, :, :].rearrange("p g t -> p (g t)"),
                                                                   start=True, stop=(msk is None)))(scb, lk, msk),
                     reads=[lkn, f'aqT{qs}'], writes=[scn], n=512)
                if msk is not None:
                    p.op('pe', (lambda scb, msk: lambda e: e.matmul(scb[:, :], lhsT=identb[:],
                                                                   rhs=maskb[:, msk, :, :].rearrange("p g t -> p (g t)"),
                                                                   start=False, stop=True))(scb, msk),
                         reads=['identb', 'maskb'], writes=[scn], n=512)
                pi = ptc[0] % 2
                ptc[0] += 1
                p.op('act', (lambda scb, pi: lambda e: e.activation(out=PT[pi][:], in_=scb[:, :], func=AF.Exp, scale=0.125))(scb, pi),
                     reads=[scn], writes=[f'PT{pi}'], n=512)
                p.op('pe', (lambda lv, pi, ki: lambda e: e.matmul(aob[:, :], lhsT=lv, rhs=PT[pi][:], start=(ki == 0), stop=False))(lv, pi, ki),
                     reads=[lvn, f'PT{pi}'], writes=[aon], n=512)
            p.op('pe', lambda e: e.matmul(aob[:, :], lhsT=sel[0:1, a, :], rhs=esr[0:1, a, :], start=False, stop=True),
                 reads=['sel', 'esr'], writes=[aon])
            p.op('act', lambda e: e.activation(out=rdn[pa, :], in_=aob[pd, :], func=AF.Ln), reads=[aon], writes=['junk'], n=512)
            p.op('act', lambda e: e.activation(out=rdn[pa, :], in_=rdn[pa, :], func=AF.Exp, scale=-1.0), reads=['junk'], writes=['junk'], n=512)
            p.op('pool', lambda e: e.tensor_tensor(out=rdn[pa, :].rearrange("p (g t) -> p g t", g=4),
                                                   in0=rdn[pa, :].rearrange("p (g t) -> p g t", g=4), in1=sag[pa, qs, :, :],
                                                   op=ALU.mult), reads=['junk', f'sag{qs}'], writes=['junk'], n=512)
            p.op('dve', lambda e: e.tensor_tensor(out=attx[pa, slot, :, :].rearrange("p g t -> p (g t)"), in0=aob[pa, :],
                                                  in1=rdn[pa, :], op=ALU.mult), reads=[aon, 'junk'], writes=[f'attx{slot}_{a}'], n=512)

        def load_xf(n):
            p.op('pool', lambda e: e.dma_start(out=xf[:, n % 2, :], in_=dr['x'][n * 128:(n + 1) * 128, :]), writes=[f'xf{n % 2}', 'xf0b'], dma=True,
                 n=1024)

        def finalize_block(n):
            rslot = n % 3
            aslot = n % 2
            xsl = n % 2
            for half in range(2):
                cs = slice(half * 512, (half + 1) * 512)
                bi_ = 2 + half
                for c in range(8):
                    if c < 4:
                        lt = retx[:, rslot, c, :]
                        rdn_ = [f'retx{rslot}']
                    else:
                        lt = attx[:, aslot, c - 4, :]
                        rdn_ = [f'attx{aslot}_0', f'attx{aslot}_1']
                    p.op('pe', (lambda lt, c, cs, bi_: lambda e: e.matmul(cb[bi_][:, :], lhsT=lt, rhs=Wo[:, c, cs],
                                                                         start=(c == 0), stop=(c == 7)))(lt, c, cs, bi_),
                         reads=rdn_ + ['Wo'], writes=[f'c{bi_}'], n=512, needs=[f'retx_done{n}'])
                p.op('dve', (lambda cs, bi_: lambda e: e.tensor_tensor(out=xf[:, xsl, cs], in0=cb[bi_][:, :], in1=xf[:, xsl, cs],
                                                                      op=ALU.add))(cs, bi_), reads=[f'c{bi_}', f'xf{xsl}'], writes=[f'xf{xsl}'], n=512)
            p.op('act', lambda e: e.activation(out=junk[:], in_=xf[:, xsl, :], func=AF.Square, accum_out=ssq2[:, n:n + 1]),
                 reads=[f'xf{xsl}'], writes=['junk', f'ssq2_{n}'], n=1024)
            p.op('act', lambda e: e.activation(out=r2[:, n:n + 1], in_=ssq2[:, n:n + 1], func=AF.Ln, bias=epst[:, 1:2], scale=1.0),
                 reads=[f'ssq2_{n}', 'epst'], writes=[f'r2_{n}'])
            p.op('act', lambda e: e.activation(out=r2[:, n:n + 1], in_=r2[:, n:n + 1], func=AF.Exp, scale=-0.5),
                 reads=[f'r2_{n}'], writes=[f'r2_{n}'])
            p.op('dve', lambda e: e.scalar_tensor_tensor(out=xf[:, xsl, :], in0=xf[:, xsl, :], scalar=r2[:, n:n + 1], in1=fnw_bc[:],
                                                         op0=ALU.mult, op1=ALU.mult),
                 reads=[f'xf{xsl}', f'r2_{n}', 'fnw_bc'], writes=[f'xf{xsl}'], n=1024)
            p.op('pool', lambda e: e.dma_start(out=out[n * 128:(n + 1) * 128, :], in_=xf[:, xsl, :]), reads=[f'xf{xsl}'],
                 writes=[f'out{n}'], dma=True, n=1024)

        def RS(s):
            for i in range(NT):
                retention_chunk(s, i)

        def AS(s):
            if s * NT - 1 >= 0:
                attention_block(s * NT - 1)
                finalize_block(s * NT - 1)
            attention_block(s * NT)
            finalize_block(s * NT)
            if s == NSB - 1:
                attention_block(NBLK - 1)
                finalize_block(NBLK - 1)

        def late_w(item):
            kc, piece = item // 2, item % 2
            rows = slice(kc * 128, (kc + 1) * 128)
            if piece == 0:
                p.op('sp', lambda e: e.dma_start(out=xf[:, 0, :], in_=dr['w_in'][rows, 1536:2560]), writes=['xf0'], dma=True, n=1024)
                p.op('act', lambda e: e.copy(out=Wb[:, kc, 1536:2048], in_=xf[:, 0, 0:512]), reads=['xf0'], writes=['WbL'], n=512)
                p.op('pool', lambda e: e.tensor_copy(out=Wb[:, kc, 2048:2560].rearrange("p (g a d) -> p g a d", g=4, a=2),
                                                     in_=xf[:, 0, 512:1024].rearrange("p (a g d) -> p g a d", a=2, g=4)),
                     reads=['xf0'], writes=['WbL', 'xf0'], n=512)
            else:
                p.op('sp', lambda e: e.dma_start(out=xf[:, 0, 0:512], in_=dr['w_in'][rows, 0:512]), writes=['xf0'], dma=True, n=512)
                p.op('sp', lambda e: e.dma_start(out=xf[:, 0, 512:1024], in_=dr['w_in'][rows, 2816:3328]), writes=['xf0b'], dma=True, n=512)
                p.op('act', lambda e: e.copy(out=Wb[:, kc, 0:512], in_=xf[:, 0, 0:512]), reads=['xf0'], writes=['WbL', 'xf0'], n=512)
                p.op('pool', lambda e: e.tensor_copy(out=Wb[:, kc, 2816:3328].rearrange("p (g a d) -> p g a d", g=4, a=2),
                                                     in_=xf[:, 0, 512:1024].rearrange("p (a g d) -> p g a d", a=2, g=4)),
                     reads=['xf0b'], writes=['WbL', 'xf0b', 'xf0'], n=512)

        def late_wo(c):
            if c < 4:
                p.op('sp', lambda e: e.dma_start(out=xf[:, 0, :], in_=dr['w_out'][c * 128:(c + 1) * 128, :]), writes=['xf0', 'xf0b'], dma=True, n=1024)
                rd = ['xf0', 'xf1']
            else:
                g = c - 4
                p.op('sp', lambda e: e.dma_start(out=xf[0:64, 0, :], in_=dr['w_out'][512 + g * 64:512 + (g + 1) * 64, :]),
                     writes=['xf0'], dma=True, n=1024)
                p.op('sp', lambda e: e.dma_start(out=xf[64:128, 0, :], in_=dr['w_out'][512 + (4 + g) * 64:512 + (5 + g) * 64, :]),
                     writes=['xf0b'], dma=True, n=1024)
                rd = ['xf0', 'xf0b', 'xf1']
            p.op('dve', lambda e: e.tensor_tensor(out=Wo[:, c, :], in0=xf[:, 0, :], in1=xf[:, 1, :], op=ALU.mult),
                 reads=rd, writes=['Wo', 'xf0', 'xf0b'], n=1024)

        orderA = list(range(NSB - 1, -1, -1))
        xsA = XSched(orderA)
        xsA.ensure(3)
        orderB = list(range(NSB))
        xsB = XSched(orderB, par=1)
        xsB.base = NBLK
        for j in range(-2, NSB):
            strs = []
            if 0 <= j + 1 < NSB:
                p.begin()
                PA(orderA[j + 1])
                strs.append(p.end())
            if j >= 0:
                p.begin()
                CA(orderA[j])
                strs.append(p.end())
            if j + 2 < NSB:
                p.begin()
                xsA.X(j + 2, c2b, 'c2')
                strs.append(p.end())
            if 0 <= j + 2 < 16:
                p.begin()
                late_w(j + 2)
                if (j + 2) % 2 == 1:
                    late_wo((j + 2) // 2)
                strs.append(p.end())
            if j == NSB - 2:
                p.begin()
                xsB.ensure(3)
                xsB.X(0, c2b, 'c2')
                strs.append(p.end())
            if j == NSB - 1:
                p.begin()
                xsB.X(1, c2b, 'c2')
                strs.append(p.end())
                p.begin()
                PB(0)
                strs.append(p.end())
            p.merge(*strs)

        for t in range(0, NSB):
            strs = []
            if t + 1 < NSB:
                p.begin()
                PB(t + 1)
                strs.append(p.end())
            p.begin()
            RS(t)
            if t + 2 < NSB:
                xsB.X(t + 2)
            strs.append(p.end())
            p.begin()
            AS(t)
            strs.append(p.end())
            p.merge(*strs)

        for (tname, getter, shape) in taps:
            p.barrier()
            tdr = nc.dram_tensor("tap_" + tname, shape, F32, kind="ExternalOutput").ap()
            tap_out[tname] = tdr
            src = getter(locals())
            stg = sb("tapst_" + tname, [128, int(np.prod(shape[1:]))], F32)
            p.op('dve', (lambda stg, src: lambda e: e.tensor_copy(out=stg[:] if len(src.shape) == 2 else stg[:].rearrange(
                "p (a b) -> p a b", a=src.shape[1]) if len(src.shape) == 3 else stg[:].rearrange(
                "p (a b c) -> p a b c", a=src.shape[1], b=src.shape[2]), in_=src))(stg, src), writes=['tapst' + tname])
            p.op('sp', (lambda stg, tdr: lambda e: e.dma_start(out=tdr.rearrange("p a -> p a") if len(tdr.shape) == 2 else tdr,
                                                              in_=stg[:]))(stg, tdr),
                 reads=['tapst' + tname], writes=['tapo' + tname], dma=True)

        p.op('sp', lambda e: e.nop(), reads=[f'out{n}' for n in range(NBLK)] + ['tapo' + t[0] for t in taps])
        stats = p.emit(st)
        print("build stats", stats, "n_ins", len(p.ins), "sbuf_left", nc.sbuf_bytes_remaining, "model_us", max(p.tfin), "nfill", p.nfill)
    return nc


_NC = {}


def make_in_maps(x, c, ctx, c_ctx, w_ada, b_ada, w_in, ret_decay_logit, ret_gn_w, att_sink, w_out, final_norm_w):
    cst = host_consts()
    f = lambda a: np.ascontiguousarray(np.asarray(a, dtype=np.float32))
    shared = dict(
        w_ada=f(w_ada[0]), b_adaT=f(np.asarray(b_ada[0]).reshape(24, 128).T), b_gate=f(np.asarray(b_ada[0])[2048:3072].reshape(1, D)),
        w_in=f(w_in[0]), w_out=f(w_out[0]), rdl=f(np.asarray(ret_decay_logit[0]).reshape(1, 8)),
        gnwT=f(np.asarray(ret_gn_w[0]).reshape(4, 128).T), sink=f(np.asarray(att_sink[0]).reshape(1, 8)),
        fnw=f(np.asarray(final_norm_w).reshape(1, D)),
        ident=cst['ident'], perm=cst['perm'], mfb=cst['mfb'], io12=cst['io12'], jvec=cst['jvec'], mask=cst['mask'],
        cmo=cst['cmo'], rtab=cst['rtab'])
    cc = np.asarray(c_ctx, dtype=np.float32).reshape(8, 128).T
    maps = []
    for b in range(x.shape[0]):
        cb = np.asarray(c[b], dtype=np.float32).reshape(8, 128).T
        m = dict(shared)
        m['x'] = f(x[b])
        m['ctx'] = f(ctx[b])
        m['cT'] = f(np.concatenate([cb, cc], axis=1))
        maps.append(m)
    return maps


def kernel(x, c, ctx, c_ctx, w_ada, b_ada, w_in, ret_decay_logit, ret_gn_w, att_sink, w_out, final_norm_w):
    if 'nc' not in _NC:
        _NC['nc'] = build()
    nc = _NC['nc']
    maps = make_in_maps(x, c, ctx, c_ctx, w_ada, b_ada, w_in, ret_decay_logit, ret_gn_w, att_sink, w_out, final_norm_w)
    res = run_bass_kernel_spmd(nc, maps, core_ids=list(range(len(maps))))
    return np.stack([np.asarray(r["out"], dtype=np.float32) for r in res.results], axis=0)
```

```python
from contextlib import ExitStack
import numpy as np
import concourse.bass as bass
import concourse.mybir as mybir
from concourse.bass_utils import run_bass_kernel_spmd

F32 = mybir.dt.float32
BF16 = mybir.dt.bfloat16
F32R = mybir.dt.float32r
ALU = mybir.AluOpType
AF = mybir.ActivationFunctionType

D = 1024
T = 4096
LC = 256
NT = 2
SB = NT * 128
NSB = T // SB
NBLK = T // 128
RQ = 5
RK = 6
EPS = 1e-6
SDK = 128 ** -0.5
ROPE_BASE = 10000.0


class Prog:
    EPOCH = 12000
    NDMA = 24
    PSUM = frozenset(['pj0', 'pj1', 'pj2', 'pm', 'c0', 'c1', 'c2', 'c3'])

    def __init__(self, nc):
        self.nc = nc
        self.eng = dict(pe=nc.tensor, act=nc.scalar, dve=nc.vector,
                        pool=nc.gpsimd, sp=nc.sync)
        self.ins = []
        self.last_w = {}
        self.readers = {}
        self.tail = {}
        self.pend_dma = []
        self.cur = None
        self.filler = None
        self.filler_from = 0
        self.nfill = 0
        self.tfin = []
        self.tstart = []
        self.crit = []
        self.efree = {}
        self.done_marks = set()

    import os as _os
    _PEF = float(_os.environ.get('PEF', '1900'))
    COST = dict(pe=(0.03, 1 / _PEF), act=(0.19, 1 / 1400.), dve=(0.16, 1 / 960.), pool=(0.1, 0.0023), sp=(0.05, 0.0))
    LAT = 0.28
    FILL_MIN = 0.08
    FILL_FLOOR = 3
    FILL_MAX = 64
    FILL_FRAC = 1.4

    def _deps(self, engine, reads, writes, extra):
        deps = set(extra)
        pr = [t for t in reads if t in self.PSUM]
        if pr:
            reads = [t for t in reads if t not in self.PSUM]
            writes = list(writes) + [t for t in pr if t not in writes]
        for t in reads:
            w = self.last_w.get(t)
            if w is not None:
                deps.add(w)
        for t in writes:
            w = self.last_w.get(t)
            if w is not None:
                deps.add(w)
            for r in self.readers.get(t, {}).values():
                deps.update(r)
        return deps, reads, writes

    def _start(self, engine, deps):
        t0 = self.efree.get(engine, 0.0)
        for d in deps:
            t = self.tfin[d] + (0.02 if self.ins[d]['e'] == engine else self.LAT)
            if t > t0:
                t0 = t
        return t0

    def op(self, engine, fn, reads=(), writes=(), dma=False, extra=(), n=256, needs=(), marks=()):
        if self.cur is not None:
            self.cur.append(dict(engine=engine, fn=fn, reads=reads, writes=writes, dma=dma, extra=extra, n=n,
                                 needs=needs, marks=marks))
            return None
        idx = len(self.ins)
        deps, reads, writes = self._deps(engine, reads, writes, extra)
        t0 = self._start(engine, deps)
        efree0 = self.efree.get(engine, 0.0)
        crit = ('eng', self.tail.get(engine) if not dma else None)
        for d in deps:
            if self.tfin[d] + (0.02 if self.ins[d]['e'] == engine else self.LAT) >= t0 - 1e-9:
                crit = ('dep', d)
        self.tstart.append(t0)
        self.crit.append(crit)
        if dma:
            self.tfin.append(t0 + 2.0 + n * 0.004)
            self.efree[engine] = t0 + 0.05
        else:
            a, b = self.COST[engine]
            self.tfin.append(t0 + a + b * n)
            self.efree[engine] = self.tfin[-1]
        self.done_marks.update(marks)
        self.ins.append(dict(e=engine, fn=fn, deps=deps, dma=dma, stall=t0 - efree0))
        for t in reads:
            rd = self.readers.setdefault(t, {})
            if dma:
                rd.setdefault('dma', []).append(idx)
            else:
                rd[engine] = [idx]
        for t in writes:
            self.last_w[t] = idx
            self.readers[t] = {}
        if dma:
            self.pend_dma.append(idx)
        else:
            self.tail[engine] = idx
        return idx

    def begin(self):
        self.cur = []

    def end(self):
        c, self.cur = self.cur, None
        return c

    def merge(self, *streams):
        heads = [0] * len(streams)
        while True:
            best = None
            rem = False
            for si, S in enumerate(streams):
                if heads[si] >= len(S):
                    continue
                rem = True
                dsc = S[heads[si]]
                if any(m not in self.done_marks for m in dsc['needs']):
                    continue
                deps, _, _ = self._deps(dsc['engine'], dsc['reads'], dsc['writes'], dsc['extra'])
                t0 = self._start(dsc['engine'], deps)
                t0 -= 2.5 * (len(S) - heads[si]) / max(1, len(S))
                if best is None or t0 < best[0]:
                    best = (t0, si)
            if not rem:
                break
            assert best is not None, "merge deadlock: unmet stream markers"
            si = best[1]
            dsc = streams[si][heads[si]]
            heads[si] += 1
            self.op(dsc['engine'], dsc['fn'], reads=dsc['reads'], writes=dsc['writes'], dma=dsc['dma'], extra=dsc['extra'],
                    n=dsc['n'], needs=(), marks=dsc['marks'])

    def barrier(self):
        tails = list(self.tail.values()) + list(self.pend_dma)
        self.pend_dma = []
        for e in ('pe', 'act', 'dve', 'pool', 'sp'):
            self.op(e, lambda eng: eng.nop(), extra=tails)

    def emit(self, stack):
        nc = self.nc
        ins = self.ins
        n = len(ins)
        has_dep = [False] * n
        for I in ins:
            for d in I['deps']:
                if ins[d]['e'] == 'pe' and I['e'] == 'pe' and not ins[d]['dma']:
                    continue
                has_dep[d] = True
        cnt = dict(pe=0, act=0, dve=0, pool=0, sp=0)
        ndma = 0
        nq = {}
        for idx, I in enumerate(ins):
            if I['dma']:
                q = 'pdma' if I['e'] == 'pool' else 'dma'
                I['q'] = q
                I['k'] = nq.get(q, 0)
                nq[q] = I['k'] + 1
                ndma += 1
            elif has_dep[idx]:
                cnt[I['e']] += 1
                I['seq'] = cnt[I['e']]
        sems = {}
        for e in ('pe', 'act', 'dve', 'pool', 'sp'):
            for ep in range(cnt[e] // self.EPOCH + 1):
                sems[(e, ep)] = stack.enter_context(nc.semaphore(f"s_{e}_{ep}"))
        P = self.NDMA
        for j in range(P):
            sems[('dma', j)] = stack.enter_context(nc.semaphore(f"s_dma_{j}"))
        for j in range(min(P, nq.get('pdma', 0))):
            sems[('pdma', j)] = stack.enter_context(nc.semaphore(f"s_pdma_{j}"))
        waited = {e: {} for e in self.eng}
        for idx, I in enumerate(ins):
            e = I['e']
            eng = self.eng[e]
            need = {}
            for d in I['deps']:
                Dd = ins[d]
                if Dd['dma']:
                    key = (Dd['q'], Dd['k'] % P)
                    val = 16 * (Dd['k'] // P + 1)
                else:
                    if Dd['e'] == e and e == 'pe':
                        continue
                    s = Dd['seq']
                    ep = (s - 1) // self.EPOCH
                    key = (Dd['e'], ep)
                    val = s - ep * self.EPOCH
                if need.get(key, 0) < val:
                    need[key] = val
            if I['dma'] and I['k'] >= P:
                key = (I['q'], I['k'] % P)
                val = 16 * (I['k'] // P)
                if need.get(key, 0) < val:
                    need[key] = val
            towait = [(key, val) for key, val in need.items() if waited[e].get(key, 0) < val]
            if e == 'pe' and self.filler is not None and idx > self.filler_from and towait:
                nf = self.FILL_FLOOR
                if I['stall'] > self.FILL_MIN:
                    nf = max(nf, min(self.FILL_MAX, int(self.FILL_FRAC * I['stall'] / 0.055)))
                for _ in range(nf):
                    self.filler(eng)
                    self.nfill += 1
            for key, val in towait:
                eng.wait_ge(sems[key], val)
                waited[e][key] = val
            bi = I['fn'](eng)
            if I['dma']:
                bi.then_inc(sems[(I['q'], I['k'] % P)], 16)
            elif has_dep[idx]:
                s = I['seq']
                ep = (s - 1) // self.EPOCH
                bi.then_inc(sems[(e, ep)], 1)
        return cnt, ndma


def rope_tables(kind):
    t = np.arange(T)
    row = (t // 64).astype(np.float64)
    col = (t % 64).astype(np.float64)
    p = np.arange(128)
    if kind == 'ret':
        half = 64
        q = p
    else:
        half = 32
        q = p % 64
    nf = half // 2
    fi = q % half
    inv = ROPE_BASE ** (-np.arange(nf, dtype=np.float64) / nf)
    ang = np.where((fi < nf)[:, None], row[None, :] * inv[np.minimum(fi, nf - 1)][:, None],
                   col[None, :] * inv[np.maximum(fi - nf, 0)][:, None])
    sign = np.where(q < half, -1.0, 1.0)[:, None]
    return np.cos(ang).astype(np.float32), (sign * np.sin(ang)).astype(np.float32)


def perm_matrix(kind):
    Pm = np.zeros((128, 128), np.float32)
    for m in range(128):
        if kind == 'ret':
            s = (m + 64) % 128
        else:
            b = (m // 64) * 64
            s = b + ((m - b) + 32) % 64
        Pm[s, m] = 1
    return Pm


_CONST = {}


def host_consts():
    if _CONST:
        return _CONST
    jj = np.arange(128)
    c = _CONST
    c['ident'] = np.eye(128, dtype=np.float32)
    c['perm'] = np.concatenate([perm_matrix('ret'), perm_matrix('att')], axis=1)
    Mf = np.maximum(jj[None, :] - jj[:, None], 0).astype(np.float32)
    Mb = np.maximum(jj[:, None] - jj[None, :], 0).astype(np.float32)
    c['mfb'] = np.concatenate([Mf, Mb], axis=1)
    io1 = np.tile((jj + 1).astype(np.float32)[None, :], (128, 1))
    io2 = np.tile((128 - jj).astype(np.float32)[None, :], (128, 1))
    c['io12'] = np.concatenate([io1, io2], axis=1)
    c['jvec'] = np.stack([127 - jj, jj, 255 - jj, 128 + jj], axis=1).astype(np.float32)
    mprev = np.where(jj[:, None] >= jj[None, :], 0.0, -30000.0).astype(np.float32)
    mnext = np.where(jj[:, None] <= jj[None, :], 0.0, -30000.0).astype(np.float32)
    c['mask'] = np.concatenate([mprev, mnext], axis=1)
    cm = (np.eye(128) - 1.0 / 128).astype(np.float32)
    om = np.full((128, 128), 1.0 / 128, np.float32)
    c['cmo'] = np.concatenate([cm, om], axis=1)
    csr, ssr = rope_tables('ret')
    csa, ssa = rope_tables('att')
    tab = np.stack([csr, ssr, csa, ssa], axis=1)
    tab = tab.reshape(128, 4, NSB, SB).transpose(0, 2, 1, 3)
    c['rtab'] = np.ascontiguousarray(tab.reshape(128, NSB * 4 * SB))
    return c


DRAM_INPUTS = [
    ("x", [T, D]), ("ctx", [LC, D]), ("cT", [128, 16]), ("w_ada", [D, 3 * D]),
    ("b_adaT", [128, 24]), ("b_gate", [1, D]), ("w_in", [D, 3328]), ("w_out", [D, D]),
    ("rdl", [1, 8]), ("gnwT", [128, 4]), ("sink", [1, 8]), ("fnw", [1, D]),
    ("ident", [128, 128]), ("perm", [128, 256]), ("mfb", [128, 256]), ("io12", [128, 256]),
    ("jvec", [128, 4]), ("mask", [128, 256]), ("cmo", [128, 256]),
    ("rtab", [128, NSB * 4 * SB]),
]


def build(taps=()):
    nc = bass.Bass("TRN2", target_bir_lowering=False)
    dr = {}
    for nme, shp in DRAM_INPUTS:
        dr[nme] = nc.dram_tensor(nme, shp, F32, kind="ExternalInput").ap()
    out = nc.dram_tensor("out", [T, D], F32, kind="ExternalOutput").ap()
    tap_out = {}
    st = ExitStack()
    with st:
        p = Prog(nc)

        def sb(name, shape, dt):
            return st.enter_context(nc.sbuf_tensor('s_' + name, shape, dt))

        def ps(name, shape, dt):
            return st.enter_context(nc.psum_tensor('p_' + name, shape, dt))

        def bcl(a, n):
            return bass.AP(a.tensor, a.offset, [list(z) for z in a.ap] + [[0, n]])

        pj = [ps("pj0", [128, 512], F32), ps("pj1", [128, 512], F32), ps("pj2", [128, 512], F32)]
        pm = ps("pm", [128, 512], F32)
        tp = pm[:, :].bitcast(BF16)
        cb = [ps(f"c{i}", [128, 512], F32) for i in range(4)]
        sc = [cb[0], cb[1]]

        Wb = sb("Wb", [128, 8, 3328], BF16)
        Wo = sb("Wo", [128, 8, 1024], BF16)
        RB = sb("RB", [128, NBLK, 4, 128], BF16)
        identb = sb("identb", [128, 128], BF16)
        permb = sb("permb", [128, 256], BF16)
        cmo = sb("cmo_r", [128, 256], F32R)
        DT = sb("DT", [128, 4, 128], F32)
        dq = sb("dq", [128, 2, 4, 128], BF16)
        maskb = sb("maskb", [128, 2, 4, 128], BF16)
        fnw_bc = sb("fnw_bc", [128, 1024], F32)
        modv = sb("modv", [128, 16, 2], F32)
        lg = sb("lg", [128, 8], F32)
        cdec = sb("cdec", [128, 8], F32)
        wk = sb("wk", [128, 4, 4], F32)
        gnw = sb("gnw", [128, 4], F32)
        esr = sb("esr", [1, 2, 512], BF16)
        sel = sb("sel", [1, 2, 128], BF16)
        cakT = sb("cakT", [128, 256], BF16)
        cav = sb("cav", [128, 2, 2, 128], BF16)
        Rf = sb("Rf", [128, 4, 128], F32)
        xf = sb("xf", [128, 2, 1024], F32)
        ssq = sb("ssq", [128, 40], F32)
        rstd = sb("rstd", [128, 40], F32)
        ssq2 = sb("ssq2", [128, NBLK], F32)
        r2 = sb("r2", [128, NBLK], F32)
        epst = sb("epst", [128, 3], F32)

        ph0 = ExitStack()
        with ph0:
            def sb0(name, shape, dt):
                return ph0.enter_context(nc.sbuf_tensor('z_' + name, shape, dt))
            onesf = sb0("onesf", [128, 128], F32)
            wst = [sb0(f"wst{i}", [128, 3328], F32) for i in range(4)]
            cT = sb0("cT", [128, 16], F32)
            scv = sb0("scv", [128, 16], F32)
            crep = sb0("crep", [128, 8, 128], F32)
            b_adaT = sb0("b_adaT", [128, 24], F32)
            bgate = sb0("bgate", [128, 1024], F32)
            gate_bc = xf[:, 1, :]
            rdl = sb0("rdl", [128, 8], F32)
            sink_sb = sb0("sink_sb", [128, 8], F32)
            esk = sb0("esk", [128, 8], F32)
            tmpc = sb0("tmpc", [128, 256], F32)
            mfb = sb0("mfb", [128, 256], F32)
            io12 = sb0("io12", [128, 256], F32)
            jvec = sb0("jvec", [128, 4], F32)
            tmp1 = sb0("tmp1", [128, 128], F32)
            tmp2 = sb0("tmp2", [128, 128], F32)
            wkt = sb0("wkt", [128, 4, 4], F32)
            small = sb0("small", [128, 8], F32)

            def ld(dst, src, name):
                p.op('sp', lambda e: e.dma_start(out=dst, in_=src), writes=[name], dma=True)

            ld(cT[:], dr['cT'][:, :], 'cT')
            ld(b_adaT[:], dr['b_adaT'][:, :], 'b_adaT')
            ld(rdl[:], dr['rdl'].partition_broadcast(128), 'rdl')
            ld(sink_sb[:], dr['sink'].partition_broadcast(128), 'sink_sb')
            ld(gnw[:], dr['gnwT'][:, :], 'gnw')
            ld(jvec[:], dr['jvec'][:, :], 'jvec')
            p.op('dve', lambda e: e.memset(onesf[:], 1.0), writes=['onesf'])
            p.op('dve', lambda e: e.memset(epst[:, 0:1], EPS), writes=['epst'])
            p.op('dve', lambda e: e.memset(epst[:, 1:2], D * EPS), writes=['epst'])
            p.op('dve', lambda e: e.memset(epst[:, 2:3], 1.0), writes=['epst'])
            p.op('act', lambda e: e.activation(out=scv[:], in_=cT[:], func=AF.Silu), reads=['cT'], writes=['scv'])
            for kc in range(8):
                p.op('dve', (lambda kc: lambda e: e.tensor_scalar(out=crep[:, kc, :], in0=onesf[:], scalar1=scv[:, kc:kc + 1],
                                                                 scalar2=None, op0=ALU.mult))(kc),
                     reads=['onesf', 'scv'], writes=['crep'])
            for kc in range(8):
                w = wst[kc % 4]
                wn = f'wst{kc % 4}'
                ld(w[:, 0:3072], dr['w_ada'][kc * 128:(kc + 1) * 128, :], wn)
                for j in range(16):
                    p.op('pe', (lambda w, j, kc: lambda e: e.matmul(pj[0][:, 2 * j:2 * j + 2], lhsT=w[:, j * 128:(j + 1) * 128],
                                                                   rhs=scv[:, kc::8], start=(kc == 0 and j == 0), stop=(kc == 7 and j == 15)))(w, j, kc),
                         reads=[wn, 'scv'], writes=['pj0'])
                for nn in range(2):
                    p.op('pe', (lambda w, nn, kc: lambda e: e.matmul(sc[nn][:, :], lhsT=crep[:, kc, :],
                                                                    rhs=w[:, 2048 + nn * 512:2048 + (nn + 1) * 512],
                                                                    start=(kc == 0), stop=(kc == 7)))(w, nn, kc),
                         reads=[wn, 'crep'], writes=[f'c{nn}'])
            p.op('dve', lambda e: e.tensor_tensor(out=modv[:, :, :], in0=pj[0][:, 0:32].rearrange("p (j t) -> p j t", t=2),
                                                  in1=bcl(b_adaT[:, 0:16], 2), op=ALU.add),
                 reads=['pj0', 'b_adaT'], writes=['modv'])
            p.op('dve', lambda e: e.tensor_scalar(out=modv[:, 8:16, :], in0=modv[:, 8:16, :], scalar1=1.0, scalar2=None, op0=ALU.add),
                 reads=['modv'], writes=['modv'])
            ld(bgate[:], dr['b_gate'].partition_broadcast(128), 'bgate')
            ld(fnw_bc[:], dr['fnw'].partition_broadcast(128), 'fnw_bc')
            for nn in range(2):
                p.op('dve', (lambda nn: lambda e: e.tensor_tensor(out=xf[:, 1, nn * 512:(nn + 1) * 512], in0=sc[nn][:, :],
                                                                 in1=bgate[:, nn * 512:(nn + 1) * 512], op=ALU.add))(nn),
                     reads=[f'c{nn}', 'bgate'], writes=['xf1'])
            p.op('dve', lambda e: e.tensor_scalar(out=fnw_bc[:], in0=fnw_bc[:], scalar1=32.0, scalar2=None, op0=ALU.mult),
                 reads=['fnw_bc'], writes=['fnw_bc'])
            p.op('act', lambda e: e.activation(out=small[:], in_=rdl[:], func=AF.Exp, scale=-1.0), reads=['rdl'], writes=['small'])
            p.op('dve', lambda e: e.tensor_scalar(out=small[:], in0=small[:], scalar1=1.0, scalar2=None, op0=ALU.add),
                 reads=['small'], writes=['small'])
            p.op('act', lambda e: e.activation(out=lg[:], in_=small[:], func=AF.Ln), reads=['small'], writes=['lg'])
            p.op('dve', lambda e: e.tensor_scalar(out=lg[:], in0=lg[:], scalar1=-1.0, scalar2=None, op0=ALU.mult),
                 reads=['lg'], writes=['lg'])
            p.op('act', lambda e: e.activation(out=cdec[:], in_=lg[:], func=AF.Exp, scale=128.0), reads=['lg'], writes=['cdec'])
            p.op('act', lambda e: e.activation(out=esk[:], in_=sink_sb[:], func=AF.Exp), reads=['sink_sb'], writes=['esk'])
            ld(mfb[:], dr['mfb'][:, :], 'mfb')
            ld(io12[:], dr['io12'][:, :], 'io12')
            for h in range(4):
                p.op('dve', (lambda h: lambda e: e.tensor_scalar(out=tmp1[:], in0=mfb[:, 0:128], scalar1=lg[:, h:h + 1], scalar2=None,
                                                                op0=ALU.mult))(h), reads=['mfb', 'lg'], writes=['tmp1'])
                p.op('dve', (lambda h: lambda e: e.scalar_tensor_tensor(out=tmp2[:], in0=mfb[:, 128:256], scalar=lg[:, 4 + h:5 + h],
                                                                       in1=tmp1[:], op0=ALU.mult, op1=ALU.add))(h),
                     reads=['mfb', 'lg', 'tmp1'], writes=['tmp2'])
                p.op('act', (lambda h: lambda e: e.activation(out=DT[:, h, :], in_=tmp2[:], func=AF.Exp))(h),
                     reads=['tmp2'], writes=['DT'])
                p.op('act', (lambda h: lambda e: e.activation(out=dq[:, 0, h, :], in_=io12[:, 0:128], func=AF.Exp,
                                                             scale=lg[:, h:h + 1]))(h), reads=['io12', 'lg'], writes=['dq'])
                p.op('act', (lambda h: lambda e: e.activation(out=dq[:, 1, h, :], in_=io12[:, 128:256], func=AF.Exp,
                                                             scale=lg[:, 4 + h:5 + h]))(h), reads=['io12', 'lg'], writes=['dq'])
                for kind in range(4):
                    col = (kind % 2) * 4 + h
                    p.op('dve', (lambda h, kind, col: lambda e: e.tensor_scalar(out=wkt[:, kind, h:h + 1], in0=jvec[:, kind:kind + 1],
                                                                               scalar1=lg[:, col:col + 1], scalar2=None,
                                                                               op0=ALU.mult))(h, kind, col),
                         reads=['jvec', 'lg'], writes=['wkt'])
            p.op('dve', lambda e: e.tensor_scalar(out=DT[:], in0=DT[:], scalar1=SDK, scalar2=None, op0=ALU.mult),
                 reads=['DT'], writes=['DT'])
            p.op('act', lambda e: e.activation(out=wk[:], in_=wkt[:], func=AF.Exp), reads=['wkt'], writes=['wk'])
            p.op('dve', lambda e: e.tensor_scalar(out=wk[:], in0=wk[:], scalar1=SDK, scalar2=None, op0=ALU.mult),
                 reads=['wk'], writes=['wk'])
            for a in range(2):
                for g in range(4):
                    c0 = a * 4 + g
                    p.op('dve', (lambda a, g, c0: lambda e: e.tensor_scalar(out=esr[0:1, a, g * 128:(g + 1) * 128], in0=onesf[0:1, :],
                                                                           scalar1=esk[0:1, c0:c0 + 1], scalar2=None,
                                                                           op0=ALU.mult))(a, g, c0),
                         reads=['onesf', 'esk'], writes=['esr'])
            p.op('dve', lambda e: e.memset(sel[:], 0.0), writes=['sel'])
            p.op('dve', lambda e: e.memset(sel[0:1, 0, 64:128], 1.0), writes=['sel'])
            p.op('dve', lambda e: e.memset(sel[0:1, 1, 0:64], 1.0), writes=['sel'])
            ld(tmpc[:, 0:128], dr['ident'][:, :], 'tmpc')
            p.op('dve', lambda e: e.tensor_copy(out=identb[:], in_=tmpc[:, 0:128]), reads=['tmpc'], writes=['identb'])
            ld(tmpc[:], dr['perm'][:, :], 'tmpc')
            p.op('dve', lambda e: e.tensor_copy(out=permb[:], in_=tmpc[:]), reads=['tmpc'], writes=['permb'])
            ld(tmpc[:], dr['cmo'][:, :], 'tmpc')
            p.op('dve', lambda e: e.tensor_copy(out=cmo[:], in_=tmpc[:]), reads=['tmpc'], writes=['cmo'])
            ld(tmpc[:], dr['mask'][:, :], 'tmpc')
            for g in range(4):
                p.op('dve', (lambda g: lambda e: e.tensor_copy(out=maskb[:, :, g, :],
                                                              in_=tmpc[:].rearrange("p (m i) -> p m i", m=2)))(g),
                     reads=['tmpc'], writes=['maskb'])
            for kc in range(8):
                w = wst[kc % 4]
                wn = f'wst{kc % 4}'
                ld(w[:, 0:1024], dr['w_in'][kc * 128:(kc + 1) * 128, 512:1536], wn)
                ld(w[:, 1024:1280], dr['w_in'][kc * 128:(kc + 1) * 128, 2560:2816], wn + 'b')
                p.op('dve', (lambda w, kc: lambda e: e.tensor_copy(out=Wb[:, kc, 512:1536], in_=w[:, 0:1024]))(w, kc),
                     reads=[wn], writes=['Wb', wn])
                p.op('act', (lambda w, kc: lambda e: e.copy(out=Wb[:, kc, 2560:2816], in_=w[:, 1024:1280]))(w, kc),
                     reads=[wn + 'b'], writes=['Wb', wn + 'b'])
            p.barrier()
            p.filler_from = len(p.ins)
            p.filler = lambda e: e.matmul(pj[2][:, 0:64], lhsT=identb[:], rhs=identb[:, 0:64], start=True, stop=True)

        xa = sb("xa", [128, 3, 1024], F32)
        xs = sb("xs", [128, NT, 1024], BF16)
        junk = sb("junk", [128, 1024], BF16)
        hT = sb("hT", [128, 2, 8, SB], BF16)
        tabs = sb("tabs", [128, 4, SB], F32)
        Xbf = sb("Xbf", [128, SB], BF16)
        t1 = sb("t1", [128, SB], F32)
        t2 = sb("t2", [128, SB], F32)
        rqT = sb("rqT", [128, 2, 4, SB], BF16)
        rkT = sb("rkT", [128, 2, 4, SB], BF16)
        rv = sb("rv", [128, 2, NT, 512], BF16)
        sgr = sb("sgr", [128, 2, 4, SB], BF16)
        qfb = sb("qfb", [128, 2, 4, 128], BF16)
        aqT = sb("aqT", [128, RQ, 4, 128], BF16)
        sag = sb("sag", [128, RQ, 4, 128], BF16)
        akT = sb("akT", [128, RK, 128], BF16)
        avg = sb("avg", [128, RK, 2, 128], BF16)
        PT = [sb(f"PT{i}", [128, 512], BF16) for i in range(2)]
        ktm = sb("ktm", [128, 512], BF16)
        vsc = sb("vsc", [128, 512], BF16)
        RFs = sb("RFs", [128, 4, 128], BF16)
        PTr = sb("PTr", [128, 4, 128], BF16)
        ysb = sb("ysb", [128, 512], F32R)
        gnt = sb("gnt", [128, 512], F32)
        Rb = gnt[:, :].rearrange("p (h d) -> p h d", h=4)
        rdn = junk[:, :].bitcast(F32)
        retx = sb("retx", [128, 3, 4, 128], BF16)
        attx = sb("attx", [128, 2, 4, 128], BF16)

        CMm = cmo[:, 0:128]
        OMm = cmo[:, 128:256]
        c0b = cb[0][:, :].bitcast(BF16)
        c2b = cb[2][:, :].bitcast(BF16)

        def load_x(src_ap, slot):
            p.op('sp', lambda e: e.dma_start(out=xa[:, slot, :], in_=src_ap), writes=[f'xr{slot}'], dma=True)

        def load_tabs(s):
            p.op('sp', lambda e: e.dma_start(out=tabs[:, :, :].rearrange("p a t -> p (a t)"),
                                             in_=dr['rtab'][:, s * 4 * SB:(s + 1) * 4 * SB]), writes=['tabs'], dma=True)

        def xphase(hb, slots, scols, ntile, mcol, tpv, tpn):
            for i in range(ntile):
                sl, cc = slots[i], scols[i]
                p.op('act', (lambda sl, cc, i: lambda e: e.activation(out=xs[:, i, :], in_=xa[:, sl, :], func=AF.Square,
                                                                     accum_out=ssq[:, cc:cc + 1]))(sl, cc, i),
                     reads=[f'xr{sl}'], writes=[f'xs{i}', f'ssq{cc}'], n=1024)
                p.op('act', (lambda cc: lambda e: e.activation(out=rstd[:, cc:cc + 1], in_=ssq[:, cc:cc + 1], func=AF.Ln,
                                                              bias=epst[:, 1:2], scale=1.0))(cc),
                     reads=[f'ssq{cc}', 'epst'], writes=[f'rstd{cc}'])
                p.op('act', (lambda cc: lambda e: e.activation(out=rstd[:, cc:cc + 1], in_=rstd[:, cc:cc + 1], func=AF.Exp,
                                                              scale=-0.5))(cc),
                     reads=[f'rstd{cc}'], writes=[f'rstd{cc}'])
                p.op('pool', (lambda sl, cc, i: lambda e: e.tensor_scalar(out=xs[:, i, :], in0=xa[:, sl, :],
                                                                         scalar1=rstd[:, cc:cc + 1], scalar2=32.0,
                                                                         op0=ALU.mult, op1=ALU.mult))(sl, cc, i),
                     reads=[f'xr{sl}', f'rstd{cc}'], writes=[f'xs{i}'], n=1024)
            for kc in range(8):
                tslot = kc % 4
                for i in range(ntile):
                    p.op('pe', (lambda kc, i, tslot: lambda e: e.transpose(
                        out=tpv[:, tslot * 256 + i * 128:tslot * 256 + (i + 1) * 128],
                        in_=xs[:, i, kc * 128:(kc + 1) * 128], identity=identb[:]))(kc, i, tslot),
                         reads=[f'xs{i}', 'identb'], writes=[tpn], n=128)
                if kc % 2 == 0:
                    p.op('act', (lambda kc, tslot: lambda e: e.activation(
                        out=hT[:, hb, kc, 0:ntile * 128], in_=tpv[:, tslot * 256:tslot * 256 + ntile * 128], func=AF.Identity,
                        bias=modv[:, kc, mcol:mcol + 1], scale=modv[:, 8 + kc, mcol:mcol + 1]))(kc, tslot),
                         reads=[tpn, 'modv'], writes=[f'hT{hb}'])
                else:
                    p.op('dve', (lambda kc, tslot: lambda e: e.tensor_scalar(
                        out=hT[:, hb, kc, 0:ntile * 128], in0=tpv[:, tslot * 256:tslot * 256 + ntile * 128],
                        scalar1=modv[:, 8 + kc, mcol:mcol + 1], scalar2=modv[:, kc, mcol:mcol + 1],
                        op0=ALU.mult, op1=ALU.add))(kc, tslot),
                         reads=[tpn, 'modv'], writes=[f'hT{hb}'])

        pjc = [0]

        def proj_fm(col0, ncols_tok, hb=0):
            b = pjc[0] % 2
            pjc[0] += 1
            for kc in range(8):
                p.op('pe', (lambda kc, b: lambda e: e.matmul(pj[b][:, 0:ncols_tok], lhsT=Wb[:, kc, col0:col0 + 128],
                                                            rhs=hT[:, hb, kc, 0:ncols_tok], start=(kc == 0), stop=(kc == 7)))(kc, b),
                     reads=['Wb', 'WbL', f'hT{hb}'], writes=[f'pj{b}'])
            return pj[b], f'pj{b}'

        def proj_tm(i, col0, ncols, hb=0):
            b = pjc[0] % 2
            pjc[0] += 1
            for kc in range(8):
                p.op('pe', (lambda kc, b: lambda e: e.matmul(pj[b][:, 0:ncols], lhsT=hT[:, hb, kc, i * 128:(i + 1) * 128],
                                                            rhs=Wb[:, kc, col0:col0 + ncols], start=(kc == 0), stop=(kc == 7)))(kc, b),
                     reads=['Wb', 'WbL', f'hT{hb}'], writes=[f'pj{b}'])
            return pj[b], f'pj{b}'

        def rope_evac(pst, pname, kind, dst_ap, dst_names, n=SB):
            k0 = 0 if kind == 'ret' else 2
            po = 0 if kind == 'ret' else 128
            p.op('dve', lambda e: e.tensor_copy(out=Xbf[:, 0:n], in_=pst[:, 0:n]), reads=[pname], writes=['Xbf'])
            p.op('pe', lambda e: e.matmul(pm[:, 0:n], lhsT=permb[:, po:po + 128], rhs=Xbf[:, 0:n], start=True, stop=True),
                 reads=['permb', 'Xbf'], writes=['pm'])
            p.op('dve', lambda e: e.tensor_tensor(out=t1[:, 0:n], in0=pst[:, 0:n], in1=tabs[:, k0, 0:n], op=ALU.mult),
                 reads=[pname, 'tabs'], writes=['t1'])
            p.op('dve', lambda e: e.tensor_tensor(out=t2[:, 0:n], in0=pm[:, 0:n], in1=tabs[:, k0 + 1, 0:n], op=ALU.mult),
                 reads=['pm', 'tabs'], writes=['t2'])
            shp = dst_ap.shape
            if len(shp) == 3:
                a_in0 = t1[:, 0:n].rearrange("p (b t) -> p b t", b=shp[1])
                a_in1 = t2[:, 0:n].rearrange("p (b t) -> p b t", b=shp[1])
            else:
                a_in0, a_in1 = t1[:, 0:n], t2[:, 0:n]
            p.op('pool', lambda e: e.tensor_tensor(out=dst_ap, in0=a_in0, in1=a_in1, op=ALU.add),
                 reads=['t1', 't2'], writes=dst_names)

        def ktrans_U(bf, i, kind_w, ub=1):
            for h in range(4):
                p.op('pe', (lambda h: lambda e: e.transpose(out=c0b[:, h * 128:(h + 1) * 128], in_=rkT[:, bf, h, i * 128:(i + 1) * 128],
                                                           identity=identb[:]))(h),
                     reads=[f'rkT{bf}', 'identb'], writes=['c0'], n=128)
            p.op('dve', lambda e: e.tensor_copy(out=ktm[:], in_=c0b[:, 0:512]), reads=['c0'], writes=['ktm'], n=256)
            p.op('dve', lambda e: e.tensor_tensor(out=vsc[:].rearrange("p (h d) -> p h d", h=4),
                                                   in0=rv[:, bf, i, :].rearrange("p (h d) -> p h d", h=4),
                                                   in1=bcl(wk[:, kind_w, :], 128), op=ALU.mult),
                 reads=[f'rv{bf}_{i}', 'wk'], writes=['vsc'])
            for h in range(4):
                p.op('pe', (lambda h: lambda e: e.matmul(cb[ub][:, h * 128:(h + 1) * 128], lhsT=ktm[:, h * 128:(h + 1) * 128],
                                                        rhs=vsc[:, h * 128:(h + 1) * 128], start=True, stop=True))(h),
                     reads=['ktm', 'vsc'], writes=[f'c{ub}'], n=128)

        def state_update(Rstate, rname, cd0, ub=1):
            p.op('pool', lambda e: e.tensor_tensor(out=Rstate, in0=Rstate, in1=bcl(cdec[:, cd0:cd0 + 4], 128), op=ALU.mult),
                 reads=[rname, 'cdec'], writes=[rname])
            p.op('dve', lambda e: e.tensor_tensor(out=Rstate.rearrange("p h d -> p (h d)"),
                                                  in0=Rstate.rearrange("p h d -> p (h d)"), in1=cb[ub][:, :], op=ALU.add),
                 reads=[rname, f'c{ub}'], writes=[rname])

        for i in range(2):
            load_x(dr['ctx'][i * 128:(i + 1) * 128, :], i)
        xphase(0, [0, 1], [32, 33], 2, 1, tp, 'pm')
        kc_sb = [ktm, vsc]
        for i in range(2):
            pst, pn = proj_tm(i, 1024, 512)
            p.op('act', (lambda i, pst: lambda e: e.copy(out=rv[:, 0, i, :], in_=pst[:, :]))(i, pst), reads=[pn], writes=[f'rv0_{i}'])
        for dirn in range(2):
            for i in range(2):
                pst, pn = proj_tm(i, 512, 512)
                kind = (2 if i == 0 else 0) if dirn == 0 else (1 if i == 0 else 3)
                p.op('dve', (lambda i, pst, kind: lambda e: e.tensor_tensor(
                    out=kc_sb[i][:].rearrange("p (h d) -> p h d", h=4), in0=pst[:, :].rearrange("p (h d) -> p h d", h=4),
                    in1=bcl(wk[:, kind, :], 128), op=ALU.mult))(i, pst, kind), reads=[pn, 'wk'], writes=['ktm' if i == 0 else 'vsc'])
            for h in range(4):
                for i in range(2):
                    p.op('pe', (lambda h, i: lambda e: e.matmul(cb[1][:, h * 128:(h + 1) * 128], lhsT=kc_sb[i][:, h * 128:(h + 1) * 128],
                                                               rhs=rv[:, 0, i, h * 128:(h + 1) * 128], start=(i == 0), stop=(i == 1)))(h, i),
                         reads=['ktm', 'vsc', 'rv0_0', 'rv0_1'], writes=['c1'])
            Rst, rn = (Rf[:], 'Rf') if dirn == 0 else (Rb, 'gnt')
            p.op('dve', (lambda Rst: lambda e: e.tensor_copy(out=Rst.rearrange("p h d -> p (h d)"), in_=cb[1][:, :]))(Rst),
                 reads=['c1'], writes=[rn])
        pst, pn = proj_fm(2560, 256)
        p.op('act', (lambda pst: lambda e: e.copy(out=cakT[:], in_=pst[:, 0:256]))(pst), reads=[pn], writes=['cakT'])
        p.op('pool', lambda e: e.memset(cav[:], 1.0), writes=['cav'])
        p.op('pool', lambda e: e.memset(avg[:], 1.0), writes=[f'avg{m}' for m in range(RK)])
        for i in range(2):
            pst, pn = proj_tm(i, 2688, 128)
            p.op('act', (lambda i, pst: lambda e: e.copy(out=cav[:, i, 0, 0:64], in_=pst[:, 0:64]))(i, pst), reads=[pn], writes=['cav'])
            p.op('dve', (lambda i, pst: lambda e: e.tensor_copy(out=cav[:, i, 1, 64:128], in_=pst[:, 64:128]))(i, pst),
                 reads=[pn], writes=['cav'])

        class XSched:
            def __init__(self, order, par=0):
                self.order = order
                self.par = par
                self.blist = [s * NT + i for s in order for i in range(NT)]
                self.nload = 0
                self.base = 0

            def ensure(self, upto):
                while self.nload < min(upto, len(self.blist)):
                    n = self.blist[self.nload]
                    load_x(dr['x'][n * 128:(n + 1) * 128, :], (self.base + self.nload) % 3)
                    self.nload += 1

            def X(self, j, tpv=None, tpn='c0'):
                s = self.order[j]
                blocks = [s * NT + i for i in range(NT)]
                xphase((s + self.par) % 2, [(self.base + NT * j + i) % 3 for i in range(NT)], blocks, NT, 0,
                       c0b if tpv is None else tpv, tpn)
                self.ensure(NT * j + 5)

        def PA(s):
            load_tabs(s)
            bf = s % 2
            for h in range(4):
                pst, pn = proj_fm(512 + h * 128, SB, bf)
                rope_evac(pst, pn, 'ret', rkT[:, bf, h, :], [f'rkT{bf}'])
            for i in range(NT):
                pst, pn = proj_tm(i, 1024, 512, bf)
                p.op('act', (lambda i, pst: lambda e: e.copy(out=rv[:, bf, i, :], in_=pst[:, :]))(i, pst), reads=[pn], writes=[f'rv{bf}_{i}'])

        def CA(s):
            bf = s % 2
            ubank = {1: 1, 0: 3}
            for i in range(NT - 1, -1, -1):
                ktrans_U(bf, i, 1, ubank[i])
            for i in range(NT - 1, -1, -1):
                n = s * NT + i
                p.op('act', (lambda n: lambda e: e.copy(out=RB[:, n, :, :], in_=Rb))(n), reads=['gnt'], writes=[f'RB{n}'], n=512)
                state_update(Rb, 'gnt', 4, ubank[i])

        def PB(s):
            load_tabs(s)
            blocks = [s * NT + i for i in range(NT)]
            bf = (s + 1) % 2
            q0 = blocks[0] % RQ
            k0 = blocks[0] % RK
            qsl = [n % RQ for n in blocks]
            ksl = [n % RK for n in blocks]

            def c_rq(h):
                pst, pn = proj_fm(h * 128, SB, bf)
                rope_evac(pst, pn, 'ret', rqT[:, bf, h, :], [f'rqT{bf}'])

            def c_rk(h):
                pst, pn = proj_fm(512 + h * 128, SB, bf)
                rope_evac(pst, pn, 'ret', rkT[:, bf, h, :], [f'rkT{bf}'])

            def c_aq(g):
                pst, pn = proj_fm(2048 + g * 128, SB, bf)
                if qsl[1] == qsl[0] + 1:
                    rope_evac(pst, pn, 'att', aqT[:, q0:q0 + NT, g, :], [f'aqT{z}' for z in qsl])
                else:
                    rope_evac2(pst, pn, 'att', [aqT[:, z, g, :] for z in qsl], [f'aqT{z}' for z in qsl])

            def c_ak():
                pst, pn = proj_fm(2560, SB, bf)
                if ksl[1] == ksl[0] + 1:
                    rope_evac(pst, pn, 'att', akT[:, k0:k0 + NT, :], [f'akT{z}' for z in ksl])
                else:
                    rope_evac2(pst, pn, 'att', [akT[:, z, :] for z in ksl], [f'akT{z}' for z in ksl])

            def c_rg(h):
                pst, pn = proj_fm(1536 + h * 128, SB, bf)
                sigmoid_to_t1(pst, pn)
                p.op('dve', (lambda h, pst: lambda e: e.scalar_tensor_tensor(out=sgr[:, bf, h, :], in0=pst[:, 0:SB], scalar=gnw[:, h:h + 1],
                                                                            in1=t1[:, 0:SB], op0=ALU.mult, op1=ALU.mult))(h, pst),
                     reads=[pn, 'gnw', 't1'], writes=[f'sgr{bf}'])

            def c_ag(g):
                pst, pn = proj_fm(2816 + g * 128, SB, bf)
                sigmoid_to_t1(pst, pn)
                for i in range(NT):
                    p.op('dve', (lambda g, pst, i, z: lambda e: e.tensor_tensor(out=sag[:, z, g, :], in0=pst[:, i * 128:(i + 1) * 128],
                                                                               in1=t1[:, i * 128:(i + 1) * 128], op=ALU.mult))(g, pst, i, qsl[i]),
                         reads=[pn, 't1'], writes=[f'sag{qsl[i]}'], n=128)

            def c_tm(i):
                pst, pn = proj_tm(i, 1024, 512, bf)
                p.op('act', (lambda i, pst: lambda e: e.copy(out=rv[:, bf, i, :], in_=pst[:, :]))(i, pst), reads=[pn], writes=[f'rv{bf}_{i}'])
                pst, pn = proj_tm(i, 2688, 128, bf)
                ms = ksl[i]
                p.op('act', (lambda ms, pst: lambda e: e.copy(out=avg[:, ms, 0, 0:64], in_=pst[:, 0:64]))(ms, pst),
                     reads=[pn], writes=[f'avg{ms}'])
                p.op('dve', (lambda ms, pst: lambda e: e.tensor_copy(out=avg[:, ms, 1, 64:128], in_=pst[:, 64:128]))(ms, pst),
                     reads=[pn], writes=[f'avg{ms}'])

            for h in range(4):
                c_rq(h); c_rg(h)
            for h in range(4):
                c_rk(h); c_ag(h)
            for g in range(4): c_aq(g)
            c_ak()
            for i in range(NT): c_tm(i)

        def sigmoid_to_t1(pst, pname):
            p.op('act', lambda e: e.activation(out=t1[:, 0:SB], in_=pst[:, 0:SB], func=AF.Exp, scale=-1.0), reads=[pname], writes=['t1'])
            p.op('act', lambda e: e.activation(out=t1[:, 0:SB], in_=t1[:, 0:SB], func=AF.Ln, bias=epst[:, 2:3], scale=1.0),
                 reads=['t1', 'epst'], writes=['t1'])
            p.op('act', lambda e: e.activation(out=t1[:, 0:SB], in_=t1[:, 0:SB], func=AF.Exp, scale=-1.0), reads=['t1'], writes=['t1'])

        def rope_evac2(pst, pname, kind, dsts, dst_names, n=SB):
            k0 = 0 if kind == 'ret' else 2
            po = 0 if kind == 'ret' else 128
            p.op('dve', lambda e: e.tensor_copy(out=Xbf[:, 0:n], in_=pst[:, 0:n]), reads=[pname], writes=['Xbf'])
            p.op('pe', lambda e: e.matmul(pm[:, 0:n], lhsT=permb[:, po:po + 128], rhs=Xbf[:, 0:n], start=True, stop=True),
                 reads=['permb', 'Xbf'], writes=['pm'])
            p.op('dve', lambda e: e.tensor_tensor(out=t1[:, 0:n], in0=pst[:, 0:n], in1=tabs[:, k0, 0:n], op=ALU.mult),
                 reads=[pname, 'tabs'], writes=['t1'])
            p.op('dve', lambda e: e.tensor_tensor(out=t2[:, 0:n], in0=pm[:, 0:n], in1=tabs[:, k0 + 1, 0:n], op=ALU.mult),
                 reads=['pm', 'tabs'], writes=['t2'])
            for i, (d_, nm_) in enumerate(zip(dsts, dst_names)):
                p.op('pool', (lambda d_, i: lambda e: e.tensor_tensor(out=d_, in0=t1[:, i * 128:(i + 1) * 128],
                                                                     in1=t2[:, i * 128:(i + 1) * 128], op=ALU.add))(d_, i),
                     reads=['t1', 't2'], writes=[nm_])

        def retention_chunk(s, i):
            n = s * NT + i
            bf = (s + 1) % 2
            slot = n % 3
            tsl = slice(i * 128, (i + 1) * 128)
            p.op('act', lambda e: e.copy(out=RFs[:], in_=Rf[:]), reads=['Rf'], writes=['RFs'], n=512)
            ktrans_U(bf, i, 0)
            state_update(Rf[:], 'Rf', 0)
            p.op('dve', lambda e: e.tensor_tensor(out=qfb[:, 0, :, :], in0=rqT[:, bf, :, tsl], in1=dq[:, 0, :, :], op=ALU.mult),
                 reads=[f'rqT{bf}', 'dq'], writes=['qf'])
            p.op('dve', lambda e: e.tensor_tensor(out=qfb[:, 1, :, :], in0=rqT[:, bf, :, tsl], in1=dq[:, 1, :, :], op=ALU.mult),
                 reads=[f'rqT{bf}', 'dq'], writes=['qb'])
            STb, stn = cb[0], 'c0'
            Yb, yn = cb[1], 'c1'
            for h in range(4):
                p.op('pe', (lambda h: lambda e: e.matmul(STb[:, h * 128:(h + 1) * 128], lhsT=rkT[:, bf, h, tsl], rhs=rqT[:, bf, h, tsl],
                                                        start=True, stop=True))(h), reads=[f'rkT{bf}', f'rqT{bf}'], writes=[stn], n=128)
            p.op('dve', lambda e: e.tensor_tensor(out=PTr[:].rearrange("p h i -> p (h i)"), in0=STb[:, :],
                                                  in1=DT[:].rearrange("p h i -> p (h i)"), op=ALU.mult),
                 reads=[stn, 'DT'], writes=['PTr'], n=512)
            for h in range(4):
                ysl = slice(h * 128, (h + 1) * 128)
                p.op('pe', (lambda h, ysl: lambda e: e.matmul(Yb[:, ysl], lhsT=rv[:, bf, i, ysl], rhs=PTr[:, h, :], start=True, stop=False))(h, ysl),
                     reads=[f'rv{bf}_{i}', 'PTr'], writes=[yn], n=128)
                p.op('pe', (lambda h, ysl: lambda e: e.matmul(Yb[:, ysl], lhsT=RFs[:, h, :], rhs=qfb[:, 0, h, :], start=False, stop=False))(h, ysl),
                     reads=['RFs', 'qf'], writes=[yn], n=128)
                p.op('pe', (lambda h, ysl: lambda e: e.matmul(Yb[:, ysl], lhsT=RB[:, n, h, :], rhs=qfb[:, 1, h, :], start=False, stop=True))(h, ysl),
                     reads=[f'RB{n}', 'qb'], writes=[yn], n=128)
            p.op('act', lambda e: e.copy(out=ysb[:], in_=Yb[:, :]), reads=[yn], writes=['ysb'], n=512)
            p.op('pe', lambda e: e.matmul(cb[0][:, :], lhsT=CMm, rhs=ysb[:], start=True, stop=True), reads=['cmo', 'ysb'], writes=['c0'], n=512)
            p.op('act', lambda e: e.activation(out=ysb[:], in_=cb[0][:, :], func=AF.Square), reads=['c0'], writes=['ysb'], n=512)
            p.op('pe', lambda e: e.matmul(cb[1][:, :], lhsT=OMm, rhs=ysb[:], start=True, stop=True), reads=['cmo', 'ysb'], writes=['c1'], n=512)
            p.op('act', lambda e: e.activation(out=gnt[:], in_=cb[1][:, :], func=AF.Ln, bias=epst[:, 0:1], scale=1.0),
                 reads=['c1', 'epst'], writes=['gnt'], n=512)
            p.op('act', lambda e: e.activation(out=gnt[:], in_=gnt[:], func=AF.Exp, scale=-0.5), reads=['gnt'], writes=['gnt'], n=512)
            p.op('dve', lambda e: e.tensor_tensor(out=gnt[:], in0=cb[0][:, :], in1=gnt[:], op=ALU.mult), reads=['c0', 'gnt'], writes=['gnt'], n=512)
            p.op('pool', lambda e: e.tensor_tensor(out=retx[:, slot, :, :], in0=gnt[:].rearrange("p (h i) -> p h i", h=4),
                                                   in1=sgr[:, bf, :, tsl], op=ALU.mult), reads=['gnt', f'sgr{bf}'], writes=[f'retx{slot}'],
                 n=512, marks=[f'retx_done{n}'])

        ptc = [0]
        scc = [0]
        aoc = [0]

        def attention_block(n):
            load_xf(n)
            for a in range(2):
                att_kv(n, a)

        def att_kv(n, a):
            slot = n % 2
            qs = n % RQ
            pa = slice(a * 64, (a + 1) * 64)
            pd = slice((1 - a) * 64, (2 - a) * 64)
            aob_i = 3
            aob, aon = cb[aob_i], f'c{aob_i}'
            kbs = [('c', 0, None), ('c', 1, None)]
            if n - 1 >= 0:
                kbs.append(('l', n - 1, 0))
            kbs.append(('l', n, None))
            if n + 1 < NBLK:
                kbs.append(('l', n + 1, 1))
            for ki, (typ, m, msk) in enumerate(kbs):
                sb_i = scc[0] % 2
                scc[0] += 1
                scb, scn = cb[2], 'c2'
                if typ == 'c':
                    lk = cakT[pa, m * 128:(m + 1) * 128]
                    lkn = 'cakT'
                    lv = cav[:, m, a, :]
                    lvn = 'cav'
                else:
                    ms = m % RK
                    lk = akT[pa, ms, :]
                    lkn = f'akT{ms}'
                    lv = avg[:, ms, a, :]
                    lvn = f'avg{ms}'
                p.op('pe', (lambda scb, lk, msk: lambda e: e.matmul(scb[:, :], lhsT=lk, rhs=aqT[pa, qs, :, :].rearrange("p g t -> p (g t)"),
                                                                   start=True, stop=(msk is None)))(scb, lk, msk),
                     reads=[lkn, f'aqT{qs}'], writes=[scn], n=512)
                if msk is not None:
                    p.op('pe', (lambda scb, msk: lambda e: e.matmul(scb[:, :], lhsT=identb[:],
                                                                   rhs=maskb[:, msk, :, :].rearrange("p g t -> p (g t)"),
                                                                   start=False, stop=True))(scb, msk),
                         reads=['identb', 'maskb'], writes=[scn], n=512)
                pi = ptc[0] % 2
                ptc[0] += 1
                p.op('act', (lambda scb, pi: lambda e: e.activation(out=PT[pi][:], in_=scb[:, :], func=AF.Exp, scale=0.125))(scb, pi),
                     reads=[scn], writes=[f'PT{pi}'], n=512)
                p.op('pe', (lambda lv, pi, ki: lambda e: e.matmul(aob[:, :], lhsT=lv, rhs=PT[pi][:], start=(ki == 0), stop=False))(lv, pi, ki),
                     reads=[lvn, f'PT{pi}'], writes=[aon], n=512)
            p.op('pe', lambda e: e.matmul(aob[:, :], lhsT=sel[0:1, a, :], rhs=esr[0:1, a, :], start=False, stop=True),
                 reads=['sel', 'esr'], writes=[aon])
            p.op('act', lambda e: e.activation(out=rdn[pa, :], in_=aob[pd, :], func=AF.Ln), reads=[aon], writes=['junk'], n=512)
            p.op('act', lambda e: e.activation(out=rdn[pa, :], in_=rdn[pa, :], func=AF.Exp, scale=-1.0), reads=['junk'], writes=['junk'], n=512)
            p.op('pool', lambda e: e.tensor_tensor(out=rdn[pa, :].rearrange("p (g t) -> p g t", g=4),
                                                   in0=rdn[pa, :].rearrange("p (g t) -> p g t", g=4), in1=sag[pa, qs, :, :],
                                                   op=ALU.mult), reads=['junk', f'sag{qs}'], writes=['junk'], n=512)
            p.op('dve', lambda e: e.tensor_tensor(out=attx[pa, slot, :, :].rearrange("p g t -> p (g t)"), in0=aob[pa, :],
                                                  in1=rdn[pa, :], op=ALU.mult), reads=[aon, 'junk'], writes=[f'attx{slot}_{a}'], n=512)

        def load_xf(n):
            p.op('sp', lambda e: e.dma_start(out=xf[:, n % 2, :], in_=dr['x'][n * 128:(n + 1) * 128, :]), writes=[f'xf{n % 2}', 'xf0b'], dma=True,
                 n=1024)

        def finalize_block(n):
            rslot = n % 3
            aslot = n % 2
            xsl = n % 2
            for half in range(2):
                cs = slice(half * 512, (half + 1) * 512)
                bi_ = 2 + half
                for c in range(8):
                    if c < 4:
                        lt = retx[:, rslot, c, :]
                        rdn_ = [f'retx{rslot}']
                    else:
                        lt = attx[:, aslot, c - 4, :]
                        rdn_ = [f'attx{aslot}_0', f'attx{aslot}_1']
                    p.op('pe', (lambda lt, c, cs, bi_: lambda e: e.matmul(cb[bi_][:, :], lhsT=lt, rhs=Wo[:, c, cs],
                                                                         start=(c == 0), stop=(c == 7)))(lt, c, cs, bi_),
                         reads=rdn_ + ['Wo'], writes=[f'c{bi_}'], n=512, needs=[f'retx_done{n}'])
                p.op('dve', (lambda cs, bi_: lambda e: e.tensor_tensor(out=xf[:, xsl, cs], in0=cb[bi_][:, :], in1=xf[:, xsl, cs],
                                                                      op=ALU.add))(cs, bi_), reads=[f'c{bi_}', f'xf{xsl}'], writes=[f'xf{xsl}'], n=512)
            p.op('act', lambda e: e.activation(out=junk[:], in_=xf[:, xsl, :], func=AF.Square, accum_out=ssq2[:, n:n + 1]),
                 reads=[f'xf{xsl}'], writes=['junk', f'ssq2_{n}'], n=1024)
            p.op('act', lambda e: e.activation(out=r2[:, n:n + 1], in_=ssq2[:, n:n + 1], func=AF.Ln, bias=epst[:, 1:2], scale=1.0),
                 reads=[f'ssq2_{n}', 'epst'], writes=[f'r2_{n}'])
            p.op('act', lambda e: e.activation(out=r2[:, n:n + 1], in_=r2[:, n:n + 1], func=AF.Exp, scale=-0.5),
                 reads=[f'r2_{n}'], writes=[f'r2_{n}'])
            p.op('dve', lambda e: e.scalar_tensor_tensor(out=xf[:, xsl, :], in0=xf[:, xsl, :], scalar=r2[:, n:n + 1], in1=fnw_bc[:],
                                                         op0=ALU.mult, op1=ALU.mult),
                 reads=[f'xf{xsl}', f'r2_{n}', 'fnw_bc'], writes=[f'xf{xsl}'], n=1024)
            p.op('pool', lambda e: e.dma_start(out=out[n * 128:(n + 1) * 128, :], in_=xf[:, xsl, :]), reads=[f'xf{xsl}'],
                 writes=[f'out{n}'], dma=True, n=1024)

        def RS(s):
            for i in range(NT):
                retention_chunk(s, i)

        def AS(s):
            if s * NT - 1 >= 0:
                attention_block(s * NT - 1)
                finalize_block(s * NT - 1)
            attention_block(s * NT)
            finalize_block(s * NT)
            if s == NSB - 1:
                attention_block(NBLK - 1)
                finalize_block(NBLK - 1)

        def late_w(item):
            kc, piece = item // 2, item % 2
            rows = slice(kc * 128, (kc + 1) * 128)
            if piece == 0:
                p.op('pool', lambda e: e.dma_start(out=xf[:, 0, :], in_=dr['w_in'][rows, 1536:2560]), writes=['xf0'], dma=True, n=1024)
                p.op('act', lambda e: e.copy(out=Wb[:, kc, 1536:2048], in_=xf[:, 0, 0:512]), reads=['xf0'], writes=['WbL'], n=512)
                p.op('pool', lambda e: e.tensor_copy(out=Wb[:, kc, 2048:2560].rearrange("p (g a d) -> p g a d", g=4, a=2),
                                                     in_=xf[:, 0, 512:1024].rearrange("p (a g d) -> p g a d", a=2, g=4)),
                     reads=['xf0'], writes=['WbL', 'xf0'], n=512)
            else:
                p.op('pool', lambda e: e.dma_start(out=xf[:, 0, 0:512], in_=dr['w_in'][rows, 0:512]), writes=['xf0'], dma=True, n=512)
                p.op('pool', lambda e: e.dma_start(out=xf[:, 0, 512:1024], in_=dr['w_in'][rows, 2816:3328]), writes=['xf0b'], dma=True, n=512)
                p.op('act', lambda e: e.copy(out=Wb[:, kc, 0:512], in_=xf[:, 0, 0:512]), reads=['xf0'], writes=['WbL', 'xf0'], n=512)
                p.op('pool', lambda e: e.tensor_copy(out=Wb[:, kc, 2816:3328].rearrange("p (g a d) -> p g a d", g=4, a=2),
                                                     in_=xf[:, 0, 512:1024].rearrange("p (a g d) -> p g a d", a=2, g=4)),
                     reads=['xf0b'], writes=['WbL', 'xf0b', 'xf0'], n=512)

        def late_wo(c):
            if c < 4:
                p.op('pool', lambda e: e.dma_start(out=xf[:, 0, :], in_=dr['w_out'][c * 128:(c + 1) * 128, :]), writes=['xf0', 'xf0b'], dma=True, n=1024)
                rd = ['xf0', 'xf1']
            else:
                g = c - 4
                p.op('pool', lambda e: e.dma_start(out=xf[0:64, 0, :], in_=dr['w_out'][512 + g * 64:512 + (g + 1) * 64, :]),
                     writes=['xf0'], dma=True, n=1024)
                p.op('pool', lambda e: e.dma_start(out=xf[64:128, 0, :], in_=dr['w_out'][512 + (4 + g) * 64:512 + (5 + g) * 64, :]),
                     writes=['xf0b'], dma=True, n=1024)
                rd = ['xf0', 'xf0b', 'xf1']
            p.op('dve', lambda e: e.tensor_tensor(out=Wo[:, c, :], in0=xf[:, 0, :], in1=xf[:, 1, :], op=ALU.mult),
                 reads=rd, writes=['Wo', 'xf0', 'xf0b'], n=1024)

        orderA = list(range(NSB - 1, -1, -1))
        xsA = XSched(orderA)
        xsA.ensure(3)
        orderB = list(range(NSB))
        xsB = XSched(orderB, par=1)
        xsB.base = NBLK
        for j in range(-2, NSB):
            strs = []
            if 0 <= j + 1 < NSB:
                p.begin()
                PA(orderA[j + 1])
                strs.append(p.end())
            if j >= 0:
                p.begin()
                CA(orderA[j])
                strs.append(p.end())
            if j + 2 < NSB:
                p.begin()
                xsA.X(j + 2, c2b, 'c2')
                strs.append(p.end())
            if 0 <= j + 2 < 16:
                p.begin()
                late_w(j + 2)
                if (j + 2) % 2 == 1:
                    late_wo((j + 2) // 2)
                strs.append(p.end())
            if j == NSB - 2:
                p.begin()
                xsB.ensure(3)
                xsB.X(0, c2b, 'c2')
                strs.append(p.end())
            if j == NSB - 1:
                p.begin()
                xsB.X(1, c2b, 'c2')
                strs.append(p.end())
                p.begin()
                PB(0)
                strs.append(p.end())
            p.merge(*strs)

        for t in range(0, NSB):
            strs = []
            if t + 1 < NSB:
                p.begin()
                PB(t + 1)
                strs.append(p.end())
            p.begin()
            RS(t)
            if t + 2 < NSB:
                xsB.X(t + 2)
            strs.append(p.end())
            p.begin()
            AS(t)
            strs.append(p.end())
            p.merge(*strs)

        for (tname, getter, shape) in taps:
            p.barrier()
            tdr = nc.dram_tensor("tap_" + tname, shape, F32, kind="ExternalOutput").ap()
            tap_out[tname] = tdr
            src = getter(locals())
            stg = sb("tapst_" + tname, [128, int(np.prod(shape[1:]))], F32)
            p.op('dve', (lambda stg, src: lambda e: e.tensor_copy(out=stg[:] if len(src.shape) == 2 else stg[:].rearrange(
                "p (a b) -> p a b", a=src.shape[1]) if len(src.shape) == 3 else stg[:].rearrange(
                "p (a b c) -> p a b c", a=src.shape[1], b=src.shape[2]), in_=src))(stg, src), writes=['tapst' + tname])
            p.op('sp', (lambda stg, tdr: lambda e: e.dma_start(out=tdr.rearrange("p a -> p a") if len(tdr.shape) == 2 else tdr,
                                                              in_=stg[:]))(stg, tdr),
                 reads=['tapst' + tname], writes=['tapo' + tname], dma=True)

        p.op('sp', lambda e: e.nop(), reads=[f'out{n}' for n in range(NBLK)] + ['tapo' + t[0] for t in taps])
        stats = p.emit(st)
        print("build stats", stats, "n_ins", len(p.ins), "sbuf_left", nc.sbuf_bytes_remaining, "model_us", max(p.tfin), "nfill", p.nfill)
    return nc


_NC = {}


def make_in_maps(x, c, ctx, c_ctx, w_ada, b_ada, w_in, ret_decay_logit, ret_gn_w, att_sink, w_out, final_norm_w):
    cst = host_consts()
    f = lambda a: np.ascontiguousarray(np.asarray(a, dtype=np.float32))
    shared = dict(
        w_ada=f(w_ada[0]), b_adaT=f(np.asarray(b_ada[0]).reshape(24, 128).T), b_gate=f(np.asarray(b_ada[0])[2048:3072].reshape(1, D)),
        w_in=f(w_in[0]), w_out=f(w_out[0]), rdl=f(np.asarray(ret_decay_logit[0]).reshape(1, 8)),
        gnwT=f(np.asarray(ret_gn_w[0]).reshape(4, 128).T), sink=f(np.asarray(att_sink[0]).reshape(1, 8)),
        fnw=f(np.asarray(final_norm_w).reshape(1, D)),
        ident=cst['ident'], perm=cst['perm'], mfb=cst['mfb'], io12=cst['io12'], jvec=cst['jvec'], mask=cst['mask'],
        cmo=cst['cmo'], rtab=cst['rtab'])
    cc = np.asarray(c_ctx, dtype=np.float32).reshape(8, 128).T
    maps = []
    for b in range(x.shape[0]):
        cb = np.asarray(c[b], dtype=np.float32).reshape(8, 128).T
        m = dict(shared)
        m['x'] = f(x[b])
        m['ctx'] = f(ctx[b])
        m['cT'] = f(np.concatenate([cb, cc], axis=1))
        maps.append(m)
    return maps


def kernel(x, c, ctx, c_ctx, w_ada, b_ada, w_in, ret_decay_logit, ret_gn_w, att_sink, w_out, final_norm_w):
    if 'nc' not in _NC:
        _NC['nc'] = build()
    nc = _NC['nc']
    maps = make_in_maps(x, c, ctx, c_ctx, w_ada, b_ada, w_in, ret_decay_logit, ret_gn_w, att_sink, w_out, final_norm_w)
    res = run_bass_kernel_spmd(nc, maps, core_ids=list(range(len(maps))))
    return np.stack([np.asarray(r["out"], dtype=np.float32) for r in res.results], axis=0)
```
